# Optimizing a Trainium2 kernel written in Bass

```python
import jax, jax.numpy as jnp
from jax import lax
import numpy as np

D_MODEL = 2048
BATCH = 4
SEQ = 2048
DEPTH = 4

CHUNK = 64
EPS = 1e-6
RET_HEADS = 8
RET_DK = 256
RET_DV = 256
RET_W = RET_HEADS * RET_DV
ROT_BASE = 10000.0
ATT_HEADS = 16
ATT_DH = 128
ATT_W = ATT_HEADS * ATT_DH
IDX_HEADS = 16
IDX_DH = 64
INDEX_TOPK = 256
Q_BLOCK = 128

SPLIT_SIZES = (RET_HEADS * RET_DK, RET_HEADS * RET_DK, RET_W, RET_W,
               ATT_W, ATT_DH, ATT_DH, ATT_W,
               IDX_HEADS * IDX_DH, IDX_DH, IDX_HEADS,
               D_MODEL, D_MODEL)
N_IN = sum(SPLIT_SIZES)

kernel_name = "hybrid_retention_dsa_gated_trunk"


def rmsnorm(x, g):
    xf = x.astype(jnp.float32)
    r = lax.rsqrt(jnp.mean(xf * xf, axis=-1, keepdims=True) + EPS)
    return (xf * r * g.astype(jnp.float32)).astype(x.dtype)


def rotate(x, cos, sin):
    half = x.shape[-1] // 2
    x1, x2 = x[..., :half], x[..., half:]
    return jnp.concatenate([x1 * cos - x2 * sin, x2 * cos + x1 * sin], axis=-1)


def retention(q, k, v):
    B, S, H, DK = q.shape
    DV = v.shape[-1]
    dt = q.dtype
    nc = S // CHUNK
    pos = jnp.arange(S, dtype=jnp.float32)
    inv = 1.0 / (ROT_BASE ** jnp.linspace(0.0, 1.0, DK // 2, dtype=jnp.float32))
    ang = pos[:, None] * inv[None, :]
    cos = jnp.cos(ang)[None, :, None, :].astype(dt)
    sin = jnp.sin(ang)[None, :, None, :].astype(dt)
    q = rotate(q, cos, sin)
    k = rotate(k, cos, sin) * jnp.asarray(DK ** -0.5, dt)
    log_g = jnp.log(1.0 - 2.0 ** (-5.0 - jnp.arange(H, dtype=jnp.float32)))
    idx = jnp.arange(CHUNK, dtype=jnp.float32)
    intra = jnp.exp(jnp.abs(idx[:, None] - idx[None, :])[None] * log_g[:, None, None]).astype(dt)
    q_dec = jnp.exp((idx[:, None] + 1.0) * log_g[None, :]).astype(dt)
    k_dec = jnp.exp((CHUNK - 1.0 - idx[:, None]) * log_g[None, :]).astype(dt)
    c_dec = jnp.exp(CHUNK * log_g).astype(dt)

    def to_chunks(a):
        return a.reshape(B, nc, CHUNK, H, a.shape[-1]).transpose(1, 0, 2, 3, 4)

    def step(state, inp):
        qc, kc, vc = inp
        s = jnp.einsum('bihd,bjhd->bhij', qc, kc) * intra[None]
        o = jnp.einsum('bhij,bjhv->bihv', s, vc)
        o = o + jnp.einsum('bihd,bhdv->bihv', qc, state) * q_dec[None, :, :, None]
        state = state * c_dec[None, :, None, None] + jnp.einsum(
            'bjhd,bjhv->bhdv', kc * k_dec[None, :, :, None], vc)
        return state, o

    state0 = jnp.zeros((B, H, DK, DV), dt)
    _, o = lax.scan(step, state0, (to_chunks(q), to_chunks(k), to_chunks(v)))
    return o.transpose(1, 0, 2, 3, 4).reshape(B, S, H, DV)


def sparse_attention(q, k, v, qi, ki, wi):
    B, S, HA, DH = q.shape
    L = k.shape[1]
    topk = min(INDEX_TOPK, L // 4)
    nb = S // Q_BLOCK
    key_chunk = jnp.arange(L) // CHUNK
    idx_scale = IDX_DH ** -0.5
    att_scale = DH ** -0.5

    def blocks(a):
        return a.reshape((B, nb, Q_BLOCK) + a.shape[2:]).swapaxes(0, 1)

    def one_block(args):
        qb, qib, wib, tb = args
        sc = jax.nn.relu(jnp.einsum('bthd,bsd->btsh', qib, ki))
        score = jnp.einsum('btsh,bth->bts', sc, wib).astype(jnp.float32) * idx_scale
        adm = key_chunk[None, :] <= (tb // CHUNK)[:, None]
        score = jnp.where(adm[None], score, -jnp.inf)
        vals, sel = lax.top_k(score, topk)
        valid = jnp.isfinite(vals)
        ks = jax.vmap(lambda a, i: a[i])(k, sel)
        vs = jax.vmap(lambda a, i: a[i])(v, sel)
        logits = jnp.einsum('bthd,btkd->bhtk', qb, ks).astype(jnp.float32) * att_scale
        logits = jnp.where(valid[:, None], logits, -jnp.inf)
        p = jax.nn.softmax(logits, axis=-1).astype(vs.dtype)
        return jnp.einsum('bhtk,btkd->bthd', p, vs)

    tpos = jnp.arange(S, dtype=jnp.int32).reshape(nb, Q_BLOCK)
    o = lax.map(one_block, (blocks(q), blocks(qi), blocks(wi), tpos))
    return o.swapaxes(0, 1).reshape(B, S, HA, DH)


def setup_inputs(seed: int = 0) -> dict:
    key = jax.random.key(seed)
    ks = jax.random.split(key, 10)
    f32 = jnp.float32
    x = jax.random.normal(ks[0], (BATCH, SEQ, D_MODEL), f32)
    norm_g = 1.0 + 0.02 * jax.random.normal(ks[1], (DEPTH, D_MODEL), f32)
    w_in = jax.random.normal(ks[2], (DEPTH, D_MODEL, N_IN), f32) * D_MODEL ** -0.5
    ret_out_g = 1.0 + 0.02 * jax.random.normal(ks[3], (DEPTH, RET_HEADS, RET_DV), f32)
    att_q_g = 1.0 + 0.02 * jax.random.normal(ks[4], (DEPTH, ATT_DH), f32)
    att_k_g = 1.0 + 0.02 * jax.random.normal(ks[5], (DEPTH, ATT_DH), f32)
    idx_k_g = 1.0 + 0.02 * jax.random.normal(ks[6], (DEPTH, IDX_DH), f32)
    w_branch_ret = jax.random.normal(ks[7], (DEPTH, RET_W, D_MODEL), f32) * RET_W ** -0.5
    w_branch_att = jax.random.normal(ks[8], (DEPTH, ATT_W, D_MODEL), f32) * ATT_W ** -0.5
    w_out = jax.random.normal(ks[9], (DEPTH, D_MODEL, D_MODEL), f32) * D_MODEL ** -0.5
    return {"x": x, "norm_g": norm_g, "w_in": w_in, "ret_out_g": ret_out_g,
            "att_q_g": att_q_g, "att_k_g": att_k_g, "idx_k_g": idx_k_g,
            "w_branch_ret": w_branch_ret, "w_branch_att": w_branch_att, "w_out": w_out}


def reference(x, norm_g, w_in, ret_out_g, att_q_g, att_k_g, idx_k_g,
              w_branch_ret, w_branch_att, w_out):
    B, S, _ = x.shape
    offsets = []
    acc = 0
    for sz in SPLIT_SIZES[:-1]:
        acc += sz
        offsets.append(acc)
    for l in range(DEPTH):
        h = rmsnorm(x, norm_g[l])
        p = h @ w_in[l]
        (rq, rk, rv, rg, aq, ak, av, ag, iq, ik, iw, ga, gb) = jnp.split(p, offsets, axis=-1)
        ro = retention(rq.reshape(B, S, RET_HEADS, RET_DK),
                       rk.reshape(B, S, RET_HEADS, RET_DK),
                       rv.reshape(B, S, RET_HEADS, RET_DV))
        ro = rmsnorm(ro, ret_out_g[l]).reshape(B, S, RET_W)
        u_ret = (ro * jax.nn.silu(rg)) @ w_branch_ret[l]
        qa = rmsnorm(aq.reshape(B, S, ATT_HEADS, ATT_DH), att_q_g[l])
        ka = rmsnorm(ak, att_k_g[l])
        ao = sparse_attention(qa, ka, av,
                              iq.reshape(B, S, IDX_HEADS, IDX_DH),
                              rmsnorm(ik, idx_k_g[l]),
                              iw * jnp.asarray(IDX_HEADS ** -0.5, iw.dtype))
        u_att = (ao.reshape(B, S, ATT_W) * jax.nn.silu(ag)) @ w_branch_att[l]
        m = jax.nn.sigmoid(ga) * u_ret + jax.nn.sigmoid(gb) * u_att
        x = x + m @ w_out[l]
    return x
```

```python
import contextlib
import numpy as np
import concourse.bass as bass
import concourse.mybir as mybir
from concourse.bass_utils import run_bass_kernel_spmd

F32 = mybir.dt.float32
BF16 = mybir.dt.bfloat16
AF = mybir.ActivationFunctionType
ALU = mybir.AluOpType
AX = mybir.AxisListType

D = 2048
KC = 16
N_IN = 17744
O_RQ, O_RK, O_RV, O_RG, O_AQ, O_AK, O_AV, O_AG, O_IQ, O_IK, O_IW, O_GA, O_GB = (
    0, 2048, 4096, 6144, 8192, 10240, 10368, 10496, 12544, 13568, 13632, 13648, 15696)
EPS = 1e-6
NIT = 20
NEG = -30000.0
ATT_SCALE = 128 ** -0.5


class Buf:
    def __init__(s, name, t):
        s.name = name
        s.t = t
        s.w = None
        s.r = {}


class Sched:
    def __init__(s, nc, stack):
        s.nc = nc
        s.stack = stack
        s.engs = {'pe': nc.tensor, 'act': nc.scalar, 'dve': nc.vector, 'pool': nc.gpsimd, 'sp': nc.sync}
        s.sem = {k: stack.enter_context(nc.semaphore('s_' + k)) for k in s.engs}
        s.cnt = {k: 0 for k in s.engs}
        s.waited = {k: {} for k in s.engs}

    def dsem(s, key):
        k = 'dma:' + key
        if k not in s.sem:
            s.sem[k] = s.stack.enter_context(s.nc.semaphore('d_' + key))
            s.cnt[k] = 0
        return k

    def wait(s, eng, tok):
        k, v = tok
        if s.waited[eng].get(k, 0) >= v:
            return
        s.engs[eng].wait_ge(s.sem[k], v)
        s.waited[eng][k] = v

    def _deps(s, eng, reads, writes):
        for b in reads:
            if b.w is not None:
                if b.w[0] == eng and eng == 'pe':
                    continue
                s.wait(eng, b.w)
        for b in writes:
            if b.w is not None and (b.w[0] != eng or eng != 'pe'):
                s.wait(eng, b.w)
            for k, v in b.r.items():
                if k != eng or eng != 'pe':
                    s.wait(eng, (k, v))

    def _mark(s, tok, reads, writes):
        for b in reads:
            b.r[tok[0]] = tok[1]
        for b in writes:
            b.w = tok
            b.r = {}

    def op(s, eng, fn, reads=(), writes=()):
        s._deps(eng, reads, writes)
        ins = fn(s.engs[eng])
        s.cnt[eng] += 1
        ins.then_inc(s.sem[eng], 1)
        s._mark((eng, s.cnt[eng]), reads, writes)

    def dma(s, q, out, in_, reads=(), writes=()):
        s._deps(q, reads, writes)
        key = (list(writes) + list(reads))[0].name
        k = s.dsem(key)
        ins = s.engs[q].dma_start(out=out, in_=in_)
        s.cnt[k] += 16
        ins.then_inc(s.sem[k], 16)
        s._mark((k, s.cnt[k]), reads, writes)

    def barrier(s):
        for k in list(s.sem):
            if k != 'sp' and s.cnt[k] > 0:
                s.wait('sp', (k, s.cnt[k]))
        s.cnt['sp'] += 1
        s.engs['sp'].sem_inc(s.sem['sp'], 1)
        for e in s.engs:
            if e != 'sp':
                s.wait(e, ('sp', s.cnt['sp']))
        for e in s.engs:
            for k in s.sem:
                s.waited[e][k] = s.cnt[k]


class Rot:
    def __init__(s, bufs):
        s.bufs = bufs
        s.i = 0

    def next(s):
        b = s.bufs[s.i % len(s.bufs)]
        s.i += 1
        return b


def build(L, TF, debug=False):
    T = TF // 2
    NT = T // 512
    NB = T // 128
    KTOP = min(256, TF // 4)
    nc = bass.Bass('TRN2', target_bir_lowering=False, num_devices=8)
    skind = 'ExternalOutput' if debug else 'Internal'

    def din(name, shape, dt=F32):
        return nc.dram_tensor(name, shape, dt, kind='ExternalInput').ap()

    def dscr(name, shape, dt=BF16):
        return nc.dram_tensor(name, shape, dt, kind=skind).ap()

    xT_in = din('xT', [D, T])
    w_in = din('w_in', [L, D, N_IN])
    w_br = din('w_br', [L, D, D])
    w_ba = din('w_ba', [L, D, D])
    w_o = din('w_o', [L, D, D])
    cs_d = din('cs', [2, 128, T])
    rmask_d = din('rmask', [128, 8 * 128])
    dq_d = din('dq', [128, 8 * 256])
    dk_d = din('dk', [128, 8])
    ident_d = din('ident', [128, 512])
    inadm_d = din('inadm', [128, 128])
    p2_d = din('p2tab', [128, NIT])
    gcol_d = din('gcol', [128, L * 16])
    gret_d = din('gret', [128, L * 2048])
    gqk_d = din('gqk', [128, L * 3])
    flags_d = din('flags', [128, 4])
    dkp_d = din('dkp', [128, 8 * NB])
    outT = nc.dram_tensor('outT', [D, T], F32, kind='ExternalOutput').ap()

    xT_s = dscr('xT_s', [D, T], F32)
    RQ = dscr('RQ', [8, 2, 128, T])
    SRG = dscr('SRG', [T, 2048])
    QA = dscr('QA', [16, 128, T])
    SAG = dscr('SAG', [16, 128, T])
    IQ = dscr('IQ', [8, 128, T])
    IW = dscr('IW', [T, 16], F32)
    SGA = dscr('SGA', [16, 128, T])
    SGB = dscr('SGB', [16, 128, T])
    ART = dscr('ART', [16, 128, T])
    AAT = dscr('AAT', [16, 128, T])
    NXR = 4480
    XO = dscr('XO', [NXR, T])
    XCH = [(0, 1024), (1024, 2048), (2048, 3072), (3072, 4096), (4096, 4480)]
    XGs = [dscr('XG%d' % k, [2 * (b - a), T]) for k, (a, b) in enumerate(XCH)]
    RK = XO[0:2048].rearrange('(h c p) t -> h c p t', h=8, c=2)
    RV = XO[2048:4096].rearrange('(t e) c -> t (e c)', e=2048 // T)
    KA, IK = XO[4096:4224], XO[4224:4352]
    AV = XO[4352:4480].rearrange('r (e c) -> (r e) c', c=128)
    RKp_ = [XGs[k][0:1024].rearrange('(h c p) t -> h c p t', h=4, c=2) for k in (0, 1)]
    RKp = [RKp_[h // 4][h % 4] for h in range(8)]
    RVp = [XGs[k][0:1024].rearrange('(t e) c -> t (e c)', e=2048 // T) for k in (2, 3)]
    KAp, IKp = XGs[4][0:128], XGs[4][128:256]
    AVp = XGs[4][256:384].rearrange('r (e c) -> (r e) c', c=128)

    gam = [1.0 - 2.0 ** (-5.0 - h) for h in range(8)]

    with contextlib.ExitStack() as st:
        S = Sched(nc, st)

        uid = [0]

        def sb(stk, name, shape, dt):
            uid[0] += 1
            return Buf(name, stk.enter_context(nc.sbuf_tensor('%s_u%d' % (name, uid[0]), shape, dt)))

        def ps(stk, name, shape, dt=F32):
            uid[0] += 1
            return Buf(name, stk.enter_context(nc.psum_tensor('%s_u%d' % (name, uid[0]), shape, dt)))

        ident = sb(st, 'ident', [128, 512], BF16)
        S.dma('pool', ident.t[:], ident_d, writes=[ident])
        ones_f = sb(st, 'ones_f', [128, 128], F32)
        S.op('dve', lambda e: e.memset(ones_f.t[:], 1.0), writes=[ones_f])
        ones_b = sb(st, 'ones_b', [128, 128], BF16)
        S.op('dve', lambda e: e.memset(ones_b.t[:], 1.0), writes=[ones_b])
        mhalf = sb(st, 'mhalf', [128, 1], F32)
        S.op('dve', lambda e: e.memset(mhalf.t[:], -0.5), writes=[mhalf])
        gcol = sb(st, 'gcol', [128, L * 16], F32)
        S.dma('sp', gcol.t[:], gcol_d, writes=[gcol])
        gqk = sb(st, 'gqk', [128, L * 3], F32)
        S.dma('sp', gqk.t[:], gqk_d, writes=[gqk])
        flags = sb(st, 'flags', [128, 4], F32)
        S.dma('sp', flags.t[:], flags_d, writes=[flags])
        csem = st.enter_context(nc.semaphore('csem'))
        ccn = [0]
        wts = Rot([sb(st, 'wt%d' % i, [128, KC, 256], BF16) for i in range(3)])

        wq = {'specs': [], 'bufs': {}, 'issued': 0, 'next': 0}

        def _issue_w(upto):
            while wq['issued'] < min(upto, len(wq['specs'])):
                i = wq['issued']
                wt = wts.next()
                for (src, c0, ncol, dst0) in wq['specs'][i]:
                    S.dma('pool', wt.t[:, :, dst0:dst0 + ncol],
                          src[:, c0:c0 + ncol].rearrange('(kc p) m -> p kc m', p=128), writes=[wt])
                wq['bufs'][i] = wt
                wq['issued'] += 1

        def next_w():
            i = wq['next']
            _issue_w(i + 3)
            wq['next'] += 1
            return wq['bufs'].pop(i)

        def layer_wspecs(l):
            win = w_in[l]
            sp = []
            for off in (O_RQ, O_RK, O_RV, O_RG, O_AQ):
                for h in range(8):
                    sp.append([(win, off + h * 256, 256, 0)])
            sp.append([(win, O_AK, 256, 0)])
            for h in range(8):
                sp.append([(win, O_AG + h * 256, 256, 0)])
            for h in range(4):
                sp.append([(win, O_IQ + h * 256, 256, 0)])
            sp.append([(win, O_IK, 64, 0), (win, O_IK, 64, 64), (win, O_IW, 16, 128)])
            for off in (O_GA, O_GB):
                for h in range(8):
                    sp.append([(win, off + h * 256, 256, 0)])
            for wsrc in (w_br[l], w_ba[l], w_o[l]):
                for h in range(8):
                    sp.append([(wsrc, h * 256, 256, 0)])
            return sp

        def load_w(src=None, c0=None, *a, **k):
            if c0 is not None:
                assert wq['specs'][wq['next']][0][1] == c0, (wq['next'], c0)
            return next_w()

        for l in range(L):
            x_src = xT_in if l == 0 else xT_s
            x_dst = outT if l == L - 1 else xT_s
            wq['specs'] = layer_wspecs(l)
            wq['bufs'] = {}
            wq['issued'] = 0
            wq['next'] = 0
            win = w_in[l]
            with contextlib.ExitStack() as p12:
                hT = sb(p12, 'hT', [128, KC, T], BF16)
                with contextlib.ExitStack() as p1:
                    xt = sb(p1, 'xt', [128, KC, 512], F32)
                    sqr = Rot([sb(p1, 'sq%d' % i, [128, 512], F32) for i in range(3)])
                    rs = sb(p1, 'rs', [128, 512], F32)
                    rstd = sb(p1, 'rstd', [128, 512], F32)
                    ssp = ps(p1, 'ssp', [128, 512])
                    for tt in range(NT):
                        tsl = slice(tt * 512, (tt + 1) * 512)
                        S.dma('sp', xt.t[:], x_src[:, tsl].rearrange('(kc p) t -> p kc t', p=128), writes=[xt])
                        for kc in range(KC):
                            sq = sqr.next()
                            S.op('act', lambda e: e.activation(out=sq.t[:], in_=xt.t[:, kc, :], func=AF.Square),
                                 reads=[xt], writes=[sq])
                            S.op('pe', lambda e: e.matmul(ssp.t[:], ones_f.t[:], sq.t[:], start=(kc == 0),
                                                         stop=(kc == KC - 1)),
                                 reads=[ones_f, sq], writes=[ssp])
                        S.op('act', lambda e: e.activation(out=rs.t[:], in_=ssp.t[:], func=AF.Sqrt,
                                                           scale=1.0 / D, bias=EPS),
                             reads=[ssp], writes=[rs])
                        S.op('dve', lambda e: e.reciprocal(out=rstd.t[:], in_=rs.t[:]), reads=[rs], writes=[rstd])
                        for kc in range(KC):
                            S.op('dve', lambda e: e.scalar_tensor_tensor(
                                out=hT.t[:, kc, tsl], in0=xt.t[:, kc, :], scalar=gcol.t[:, l * 16 + kc:l * 16 + kc + 1],
                                in1=rstd.t[:], op0=ALU.mult, op1=ALU.mult), reads=[xt, gcol, rstd], writes=[hT])
                S.barrier()
                with contextlib.ExitStack() as p2:
                    cs = sb(p2, 'cs', [128, 2, T], F32)
                    S.dma('sp', cs.t[:], cs_d.rearrange('c p t -> p c t'), writes=[cs])
                    pb = Rot([ps(p2, 'pb%d' % i, [128, 512]) for i in range(6)])
                    ssq = ps(p2, 'ssq', [128, 512])
                    tmpf = Rot([sb(p2, 'tf%d' % i, [128, 512], F32) for i in range(8)])
                    outb = Rot([sb(p2, 'ob%d' % i, [128, 512], BF16) for i in range(6)])
                    outf = Rot([sb(p2, 'of%d' % i, [128, 16], F32) for i in range(2)])

                    def fm_tile(wt, mh, tt, pbuf):
                        tsl = slice(tt * 512, (tt + 1) * 512)
                        for kc in range(KC):
                            S.op('pe', lambda e: e.matmul(pbuf.t[:], wt.t[:, kc, mh * 128:(mh + 1) * 128],
                                                         hT.t[:, kc, tsl], start=(kc == 0), stop=(kc == KC - 1)),
                                 reads=[wt, hT], writes=[pbuf])

                    def tm_tile(wt, c0, ncol, tb, pbuf):
                        for kc in range(KC):
                            S.op('pe', lambda e: e.matmul(pbuf.t[:, 0:ncol], hT.t[:, kc, tb * 128:(tb + 1) * 128],
                                                         wt.t[:, kc, c0:c0 + ncol], start=(kc == 0),
                                                         stop=(kc == KC - 1)),
                                 reads=[wt, hT], writes=[pbuf])

                    def store(dst, ob, src_ap=None):
                        S.dma('sp', dst, ob.t[:] if src_ap is None else src_ap, reads=[ob])

                    def fm_act(wt, mh, func, dst_fn):
                        for tt in range(NT):
                            pbuf = pb.next()
                            fm_tile(wt, mh, tt, pbuf)
                            ob = outb.next()
                            S.op('act', lambda e: e.activation(out=ob.t[:], in_=pbuf.t[:], func=func),
                                 reads=[pbuf], writes=[ob])
                            store(dst_fn(tt), ob)

                    def fm_norm(wt, mh, gidx, nmean, dst_fn):
                        for tt in range(NT):
                            pbuf = pb.next()
                            fm_tile(wt, mh, tt, pbuf)
                            sq = tmpf.next()
                            S.op('act', lambda e: e.activation(out=sq.t[:], in_=pbuf.t[:], func=AF.Square),
                                 reads=[pbuf], writes=[sq])
                            S.op('pe', lambda e: e.matmul(ssq.t[:], ones_f.t[:], sq.t[:], start=True, stop=True),
                                 reads=[ones_f, sq], writes=[ssq])
                            r = tmpf.next()
                            S.op('act', lambda e: e.activation(out=r.t[:], in_=ssq.t[:], func=AF.Sqrt,
                                                               scale=1.0 / nmean, bias=EPS),
                                 reads=[ssq], writes=[r])
                            ri = tmpf.next()
                            S.op('dve', lambda e: e.reciprocal(out=ri.t[:], in_=r.t[:]), reads=[r], writes=[ri])
                            ob = outb.next()
                            S.op('dve', lambda e: e.scalar_tensor_tensor(
                                out=ob.t[:], in0=pbuf.t[:], scalar=gqk.t[:, l * 3 + gidx:l * 3 + gidx + 1],
                                in1=ri.t[:], op0=ALU.mult, op1=ALU.mult), reads=[pbuf, gqk, ri], writes=[ob])
                            store(dst_fn(tt), ob)

                    def tm_act(wt, c0, ncol, func, dst_fn):
                        for tb in range(NB):
                            pbuf = pb.next()
                            tm_tile(wt, c0, ncol, tb, pbuf)
                            ob = outb.next()
                            S.op('act', lambda e: e.activation(out=ob.t[:, 0:ncol], in_=pbuf.t[:, 0:ncol], func=func),
                                 reads=[pbuf], writes=[ob])
                            store(dst_fn(tb), ob, ob.t[:, 0:ncol])

                    for (off, dst) in ((O_RQ, RQ), (O_RK, RK)):
                        for h in range(8):
                            wt = load_w(win, off + h * 256)
                            for tt in range(NT):
                                tsl = slice(tt * 512, (tt + 1) * 512)
                                p1b = pb.next()
                                fm_tile(wt, 0, tt, p1b)
                                p2b = pb.next()
                                fm_tile(wt, 1, tt, p2b)
                                a, b, c, d = tmpf.next(), tmpf.next(), tmpf.next(), tmpf.next()
                                S.op('dve', lambda e: e.tensor_tensor(out=a.t[:], in0=p1b.t[:], in1=cs.t[:, 0, tsl], op=ALU.mult),
                                     reads=[p1b, cs], writes=[a])
                                S.op('dve', lambda e: e.tensor_tensor(out=b.t[:], in0=p2b.t[:], in1=cs.t[:, 1, tsl], op=ALU.mult),
                                     reads=[p2b, cs], writes=[b])
                                S.op('dve', lambda e: e.tensor_tensor(out=c.t[:], in0=p2b.t[:], in1=cs.t[:, 0, tsl], op=ALU.mult),
                                     reads=[p2b, cs], writes=[c])
                                S.op('dve', lambda e: e.tensor_tensor(out=d.t[:], in0=p1b.t[:], in1=cs.t[:, 1, tsl], op=ALU.mult),
                                     reads=[p1b, cs], writes=[d])
                                o1, o2 = outb.next(), outb.next()
                                S.op('pool', lambda e: e.tensor_tensor(out=o1.t[:], in0=a.t[:], in1=b.t[:], op=ALU.subtract),
                                     reads=[a, b], writes=[o1])
                                S.op('pool', lambda e: e.tensor_tensor(out=o2.t[:], in0=c.t[:], in1=d.t[:], op=ALU.add),
                                     reads=[c, d], writes=[o2])
                                store(dst[h, 0, :, tsl], o1)
                                store(dst[h, 1, :, tsl], o2)
                    for (off, dst, func) in ((O_RV, RV, AF.Copy), (O_RG, SRG, AF.Silu)):
                        for h in range(8):
                            wt = load_w(win, off + h * 256)
                            tm_act(wt, 0, 256, func, lambda tb: dst[tb * 128:(tb + 1) * 128, h * 256:(h + 1) * 256])
                    for hp in range(8):
                        wt = load_w(win, O_AQ + hp * 256)
                        for mh in range(2):
                            hh = hp * 2 + mh
                            fm_norm(wt, mh, 0, 128.0, lambda tt: QA[hh, :, tt * 512:(tt + 1) * 512])
                    wt = load_w(win, O_AK)
                    fm_norm(wt, 0, 1, 128.0, lambda tt: KA[:, tt * 512:(tt + 1) * 512])
                    tm_act(wt, 128, 128, AF.Copy, lambda tb: AV[tb * 128:(tb + 1) * 128, :])
                    for hp in range(8):
                        wt = load_w(win, O_AG + hp * 256)
                        for mh in range(2):
                            hh = hp * 2 + mh
                            fm_act(wt, mh, AF.Silu, lambda tt: SAG[hh, :, tt * 512:(tt + 1) * 512])
                    for pp in range(4):
                        wt = load_w(win, O_IQ + pp * 256)
                        for mh in range(2):
                            pr = pp * 2 + mh
                            fm_act(wt, mh, AF.Copy, lambda tt: IQ[pr, :, tt * 512:(tt + 1) * 512])
                    wt = load_w()
                    fm_norm(wt, 0, 2, 128.0, lambda tt: IK[:, tt * 512:(tt + 1) * 512])
                    for tb in range(NB):
                        pbuf = pb.next()
                        tm_tile(wt, 128, 16, tb, pbuf)
                        of = outf.next()
                        S.op('act', lambda e: e.activation(out=of.t[:], in_=pbuf.t[:, 0:16], func=AF.Copy),
                             reads=[pbuf], writes=[of])
                        S.dma('sp', IW[tb * 128:(tb + 1) * 128, :], of.t[:], reads=[of])
                    S.barrier()
                    for k, (ra, rb) in enumerate(XCH):
                        cc = nc.gpsimd.collective_compute('AllGather', ALU.bypass, replica_groups=[[0, 1], [2, 3], [4, 5], [6, 7]],
                                                          ins=[XO[ra:rb]], outs=[XGs[k]])
                        ccn[0] += 1
                        cc.then_inc(csem, 1)
                    for (off, dst) in ((O_GA, SGA), (O_GB, SGB)):
                        for hp in range(8):
                            wt = load_w(win, off + hp * 256)
                            for mh in range(2):
                                cc = hp * 2 + mh
                                fm_act(wt, mh, AF.Sigmoid, lambda tt: dst[cc, :, tt * 512:(tt + 1) * 512])
            S.barrier()
            nc.gpsimd.wait_ge(csem, ccn[0])
            S.cnt['pool'] += 1
            nc.gpsimd.sem_inc(S.sem['pool'], 1)
            S.barrier()
            with contextlib.ExitStack() as p3:
                rmask = sb(p3, 'rmask', [128, 8, 128], F32)
                S.dma('sp', rmask.t[:], rmask_d.rearrange('p (h i) -> p h i', h=8), writes=[rmask])
                dq = sb(p3, 'dq', [128, 8, 256], BF16)
                S.dma('pool', dq.t[:], dq_d.rearrange('p (h i) -> p h i', h=8), writes=[dq])
                dk = sb(p3, 'dk', [128, 8], F32)
                S.dma('sp', dk.t[:], dk_d, writes=[dk])
                gret = sb(p3, 'gret', [128, 2048], F32)
                S.dma('sp', gret.t[:], gret_d[:, l * 2048:(l + 1) * 2048], writes=[gret])
                qTs = Rot([sb(p3, 'qT%d' % i, [128, 2, T], BF16) for i in range(2)])
                kTs = Rot([sb(p3, 'kT%d' % i, [128, 2, T], BF16) for i in range(2)])
                vs = Rot([sb(p3, 'v%d' % i, [128, NB, 256], BF16) for i in range(2)])
                srgs = Rot([sb(p3, 'srg%d' % i, [128, NB, 256], BF16) for i in range(2)])
                kTps = Rot([sb(p3, 'kTp%d' % i, [128, 2, T], BF16) for i in range(2)])
                vps = Rot([sb(p3, 'vp%d' % i, [128, NB, 256], BF16) for i in range(2)])
                dkp = sb(p3, 'dkp', [128, 8 * NB], F32)
                S.dma('sp', dkp.t[:], dkp_d, writes=[dkp])
                sTs = Rot([sb(p3, 'sT%d' % i, [128, 128], BF16) for i in range(2)])
                qds = Rot([sb(p3, 'qd%d' % i, [128, 256], BF16) for i in range(2)])
                kds = Rot([sb(p3, 'kd%d' % i, [128, 256], BF16) for i in range(2)])
                S32s = Rot([sb(p3, 'S32_%d' % i, [128, 512], F32) for i in range(2)])
                kdps = Rot([sb(p3, 'kdp%d' % i, [128, 256], BF16) for i in range(2)])
                Sbs = Rot([sb(p3, 'Sb%d' % i, [128, 2, 256], BF16) for i in range(4)])
                osbs = Rot([sb(p3, 'osb%d' % i, [128, 256], F32) for i in range(2)])
                onbs = Rot([sb(p3, 'onb%d' % i, [128, 256], F32) for i in range(2)])
                abs_ = Rot([sb(p3, 'ab%d' % i, [128, 256], BF16) for i in range(2)])
                sss = Rot([sb(p3, 'ss%d' % i, [128, 1], F32) for i in range(2)])
                s2s = Rot([sb(p3, 's2%d' % i, [128, 1], F32) for i in range(2)])
                rsds = Rot([sb(p3, 'rsd%d' % i, [128, 1], F32) for i in range(2)])
                aTst = Rot([sb(p3, 'aTst%d' % i, [128, 2, 512], BF16) for i in range(2)])
                sT_ps = Rot([ps(p3, 'sTp0', [128, 512])])
                kd_ps = Rot([ps(p3, 'kdp_0', [128, 1024], BF16)])
                o_ps = Rot([ps(p3, 'op%d' % i, [128, 512]) for i in range(2)])
                aT_ps = ps(p3, 'aTp', [128, 1024], BF16)
                S_ps = ps(p3, 'Sp', [128, 512])
                pa_ps = ps(p3, 'pa', [128, 512])
                kdp_ps = ps(p3, 'kdpp', [128, 1024], BF16)

                def make_prefix(hh):
                    kTp, vp = kTps.next(), vps.next()
                    S.dma('sp', kTp.t[:], RKp[hh].rearrange('c p t -> p c t'), writes=[kTp])
                    for hf in range(2):
                        S.dma('sp', vp.t[:, hf * (NB // 2):(hf + 1) * (NB // 2), :],
                              RVp[hf][:, hh * 256:(hh + 1) * 256].rearrange('(b p) c -> p b c', p=128), writes=[vp])
                    S32h = S32s.next()
                    Sbh = Sbs.next()
                    res = {'S32': S32h, 'Sb0': Sbh}

                    def step(blk):
                        bsl = slice(blk * 128, (blk + 1) * 128)
                        for c in range(2):
                            S.op('pe', lambda e: e.transpose(kdp_ps.t[:, c * 128:(c + 1) * 128], kTp.t[:, c, bsl], ident.t[:, 0:128]),
                                 reads=[kTp, ident], writes=[kdp_ps])
                        kdp = kdps.next()
                        S.op('act', lambda e: e.activation(out=kdp.t[:], in_=kdp_ps.t[:, 0:256], func=AF.Identity,
                                                           scale=dkp.t[:, hh * NB + blk:hh * NB + blk + 1]),
                             reads=[kdp_ps, dkp], writes=[kdp])
                        for c in range(2):
                            S.op('pe', lambda e: e.matmul(pa_ps.t[:, c * 256:(c + 1) * 256], kdp.t[:, c * 128:(c + 1) * 128], vp.t[:, blk, :],
                                                         start=(blk == 0 and c == 0), stop=(blk == NB - 1), skip_group_check=True),
                                 reads=[kdp, vp], writes=[pa_ps])

                    def final():
                        for c in range(2):
                            S.op('dve', lambda e: e.tensor_scalar(out=S32h.t[:, c * 256:(c + 1) * 256], in0=pa_ps.t[:, c * 256:(c + 1) * 256],
                                                                  scalar1=flags.t[:, 0:1], scalar2=None, op0=ALU.mult),
                                 reads=[pa_ps, flags], writes=[S32h])
                        S.op('act', lambda e: e.activation(out=Sbh.t[:].rearrange('p c v -> p (c v)'), in_=S32h.t[:], func=AF.Copy),
                             reads=[S32h], writes=[Sbh])
                    res['steps'] = [(lambda blk=blk: step(blk)) for blk in range(NB)]
                    res['final'] = final
                    return res

                pend_prefix = None
                for h in range(8):
                    qT, kT, v, srg = qTs.next(), kTs.next(), vs.next(), srgs.next()
                    S.dma('sp', qT.t[:], RQ[h].rearrange('c p t -> p c t'), writes=[qT])
                    S.dma('sp', kT.t[:], RK[h].rearrange('c p t -> p c t'), writes=[kT])
                    S.dma('sp', v.t[:], RV[:, h * 256:(h + 1) * 256].rearrange('(b p) c -> p b c', p=128), writes=[v])
                    S.dma('sp', srg.t[:], SRG[:, h * 256:(h + 1) * 256].rearrange('(b p) c -> p b c', p=128), writes=[srg])
                    c2 = gam[h] ** 128
                    if h == 0:
                        pend_prefix = make_prefix(0)
                        for f in pend_prefix['steps']:
                            f()
                        pend_prefix['final']()
                    S32, Sb0 = pend_prefix['S32'], pend_prefix['Sb0']
                    nxt_prefix = make_prefix(h + 1) if h + 1 < 8 else None

                    def front(b):
                        bsl = slice(b * 128, (b + 1) * 128)
                        sp_, kp_ = sT_ps.next(), kd_ps.next()
                        sT, qd, kd = sTs.next(), qds.next(), kds.next()
                        for c in range(2):
                            S.op('pe', lambda e: e.matmul(sp_.t[:, 0:128], kT.t[:, c, bsl], qT.t[:, c, bsl], start=(c == 0), stop=(c == 1)),
                                 reads=[kT, qT], writes=[sp_])
                        for c in range(2):
                            S.op('pe', lambda e: e.transpose(kp_.t[:, c * 128:(c + 1) * 128], kT.t[:, c, bsl], ident.t[:, 0:128]),
                                 reads=[kT, ident], writes=[kp_])
                        S.op('dve', lambda e: e.tensor_tensor(out=sT.t[:], in0=sp_.t[:, 0:128], in1=rmask.t[:, h, :], op=ALU.mult),
                             reads=[sp_, rmask], writes=[sT])
                        S.op('pool', lambda e: e.tensor_tensor(out=qd.t[:].rearrange('p (c i) -> p c i', c=2), in0=qT.t[:, :, bsl],
                                                               in1=dq.t[:, h, :].rearrange('p (c i) -> p c i', c=2), op=ALU.mult),
                             reads=[qT, dq], writes=[qd])
                        S.op('act', lambda e: e.activation(out=kd.t[:], in_=kp_.t[:, 0:256], func=AF.Identity, scale=dk.t[:, h:h + 1]),
                             reads=[kp_, dk], writes=[kd])
                        return sT, qd, kd

                    state = {'Sb': Sb0}

                    def mid_back(b, sT, qd, kd, stage_buf):
                        op_ = o_ps.next()
                        Sb = state['Sb']
                        osb, onb, ss, s2, rsd = osbs.next(), onbs.next(), sss.next(), s2s.next(), rsds.next()
                        if b < NB - 1:
                            for c in range(2):
                                S.op('pe', lambda e: e.matmul(S_ps.t[:, c * 256:(c + 1) * 256], kd.t[:, c * 128:(c + 1) * 128], v.t[:, b, :], start=True, stop=True),
                                     reads=[kd, v], writes=[S_ps])
                        S.op('pe', lambda e: e.matmul(op_.t[:, 0:256], sT.t[:], v.t[:, b, :], start=True, stop=False),
                             reads=[sT, v], writes=[op_])
                        for c in range(2):
                            S.op('pe', lambda e: e.matmul(op_.t[:, 0:256], qd.t[:, c * 128:(c + 1) * 128], Sb.t[:, c, :], start=False, stop=(c == 1)),
                                 reads=[qd, Sb], writes=[op_])
                        if b < NB - 1:
                            S.op('dve', lambda e: e.scalar_tensor_tensor(out=S32.t[:], in0=S32.t[:], scalar=c2, in1=S_ps.t[:],
                                                                         op0=ALU.mult, op1=ALU.add),
                                 reads=[S32, S_ps], writes=[S32])
                            Sb2 = Sbs.next()
                            S.op('act', lambda e: e.activation(out=Sb2.t[:].rearrange('p c v -> p (c v)'), in_=S32.t[:], func=AF.Copy),
                                 reads=[S32], writes=[Sb2])
                            state['Sb'] = Sb2
                        if state.get('tail') is not None:
                            state['tail']()
                            state['tail'] = None
                        ab = abs_.next()
                        S.op('act', lambda e: e.activation(out=osb.t[:], in_=op_.t[:, 0:256], func=AF.Copy), reads=[op_], writes=[osb])
                        S.op('act', lambda e: e.activation(out=onb.t[:], in_=osb.t[:], func=AF.Square, accum_out=ss.t[:]),
                             reads=[osb], writes=[onb, ss])
                        S.op('dve', lambda e: e.tensor_scalar(out=s2.t[:], in0=ss.t[:], scalar1=1.0 / 256, scalar2=EPS, op0=ALU.mult, op1=ALU.add),
                             reads=[ss], writes=[s2])
                        S.op('pool', lambda e: e.tensor_tensor(out=rsd.t[:], in0=s2.t[:], in1=mhalf.t[:], op=ALU.pow),
                             reads=[s2, mhalf], writes=[rsd])
                        S.op('dve', lambda e: e.scalar_tensor_tensor(out=onb.t[:], in0=osb.t[:], scalar=rsd.t[:], in1=gret.t[:, h * 256:(h + 1) * 256],
                                                                     op0=ALU.mult, op1=ALU.mult),
                             reads=[osb, rsd, gret], writes=[onb])
                        S.op('pool', lambda e: e.tensor_tensor(out=ab.t[:], in0=onb.t[:], in1=srg.t[:, b, :], op=ALU.mult),
                             reads=[onb, srg], writes=[ab])

                        def tail(b=b, ab=ab, stage_buf=stage_buf, h=h):
                            for c in range(2):
                                S.op('pe', lambda e: e.transpose(aT_ps.t[:, c * 128:(c + 1) * 128], ab.t[:, c * 128:(c + 1) * 128], ident.t[:, 0:128]),
                                     reads=[ab, ident], writes=[aT_ps])
                            bb = b % 4
                            S.op('act', lambda e: e.activation(out=stage_buf.t[:, :, bb * 128:(bb + 1) * 128],
                                                               in_=aT_ps.t[:, 0:256].rearrange('p (c i) -> p c i', c=2), func=AF.Copy),
                                 reads=[aT_ps], writes=[stage_buf])
                            if bb == 3:
                                t0 = (b // 4) * 512
                                S.dma('sp', ART[2 * h:2 * h + 2, :, t0:t0 + 512].rearrange('c p t -> p c t'), stage_buf.t[:], reads=[stage_buf])
                        state['tail'] = tail

                    nxt = front(0)
                    stage_buf = None
                    for b in range(NB):
                        cur = nxt
                        if b + 1 < NB:
                            nxt = front(b + 1)
                        if b % 4 == 0:
                            stage_buf = aTst.next()
                        mid_back(b, cur[0], cur[1], cur[2], stage_buf)
                        if nxt_prefix is not None:
                            nxt_prefix['steps'][b]()
                    state['tail']()
                    state['tail'] = None
                    if nxt_prefix is not None:
                        nxt_prefix['final']()
                    pend_prefix = nxt_prefix
            S.barrier()
            with contextlib.ExitStack() as p4:
                kaT = sb(p4, 'kaT', [128, TF], BF16)
                S.dma('sp', kaT.t[:, 0:T], KAp, writes=[kaT])
                S.dma('sp', kaT.t[:, T:TF], KA, writes=[kaT])
                ikT = sb(p4, 'ikT', [128, TF], BF16)
                S.dma('sp', ikT.t[:, 0:T], IKp, writes=[ikT])
                S.dma('sp', ikT.t[:, T:TF], IK, writes=[ikT])
                av = sb(p4, 'av', [128, 2 * NB, 128], BF16)
                S.dma('sp', av.t[:, 0:NB, :], AVp.rearrange('(b p) c -> p b c', p=128), writes=[av])
                S.dma('sp', av.t[:, NB:2 * NB, :], AV.rearrange('(b p) c -> p b c', p=128), writes=[av])
                inadm = sb(p4, 'inadm', [128, 128], F32)
                S.dma('sp', inadm.t[:], inadm_d, writes=[inadm])
                p2t = sb(p4, 'p2t', [128, NIT], F32)
                S.dma('sp', p2t.t[:], p2_d, writes=[p2t])
                qaTs = Rot([sb(p4, 'qaT%d' % i, [128, 16, 128], BF16) for i in range(2)])
                iqTs = Rot([sb(p4, 'iqz%d' % i, [128, 2, 8, 128], BF16) for i in range(2)])
                for _b in iqTs.bufs:
                    S.op('dve', lambda e: e.memset(_b.t[:], 0.0), writes=[_b])
                iws = Rot([sb(p4, 'iw%d' % i, [128, 16], F32) for i in range(2)])
                sags = Rot([sb(p4, 'sag%d' % i, [128, 16, 128], BF16) for i in range(2)])
                Dg = sb(p4, 'Dg', [128, 16, 128], BF16)
                score = sb(p4, 'score', [128, TF], F32)
                junk = sb(p4, 'junk', [128, TF], BF16)
                negm = sb(p4, 'negm', [128, TF], BF16)
                rl = Rot([sb(p4, 'rl%d' % i, [128, 512], BF16) for i in range(4)])
                pTs = Rot([sb(p4, 'pT%d' % i, [128, 512], BF16) for i in range(4)])
                mx = sb(p4, 'mx', [128, 1], F32)
                mn = sb(p4, 'mn', [128, 1], F32)
                rng = sb(p4, 'rng', [128, 1], F32)
                th = sb(p4, 'th', [128, 1], F32)
                steps = sb(p4, 'steps', [128, NIT], F32)
                cnt = sb(p4, 'cnt', [128, 1], F32)
                sg = sb(p4, 'sg', [128, 1], F32)
                rd = sb(p4, 'rd', [128, 512], F32)
                tq = sb(p4, 'tq', [128, 512], F32)
                aast = Rot([sb(p4, 'aast%d' % i, [128, 16, 512], BF16) for i in range(2)])
                sl_ps = Rot([ps(p4, 'slp%d' % i, [128, 512]) for i in range(4)])
                acc_ps = Rot([ps(p4, 'accp%d' % i, [128, 512]) for i in range(2)])
                oo_ps = [ps(p4, 'oop%d' % i, [128, 512]) for i in range(1)]
                dd_ps = [ps(p4, 'ddp%d' % i, [128, 512]) for i in range(1)]

                negms = Rot([negm, sb(p4, 'negm2', [128, TF], BF16)])
                mn2 = sb(p4, 'mn2', [128, 1], F32)
                negm_of = {}

                def idx_phase(qb):
                    qsl = slice(qb * 128, (qb + 1) * 128)
                    Lk = T + (qb + 1) * 128
                    iqT, iw = iqTs.next(), iws.next()
                    ng = negms.next()
                    negm_of[qb] = ng
                    S.dma('sp', iqT.t[0:64, 0, :, :], IQ[:, 0:64, qsl].rearrange('h p t -> p h t'), writes=[iqT])
                    S.dma('sp', iqT.t[64:128, 1, :, :], IQ[:, 64:128, qsl].rearrange('h p t -> p h t'), writes=[iqT])
                    S.dma('sp', iw.t[:], IW[qsl, :], writes=[iw])
                    for hh in range(16):
                        S.op('pool', lambda e: e.tensor_scalar(out=Dg.t[:, hh, :], in0=ident.t[:, 0:128], scalar1=iw.t[:, hh:hh + 1],
                                                               scalar2=0.0, op0=ALU.mult, op1=ALU.add),
                             reads=[ident, iw], writes=[Dg])
                    for k0 in range(0, Lk, 512):
                        wk = min(512, Lk - k0)
                        acc = acc_ps.next()

                        def sc_mm(hh):
                            scp = sl_ps.next()
                            S.op('pe', lambda e: e.matmul(scp.t[:, 0:wk], iqT.t[:, hh % 2, hh // 2, :], ikT.t[:, k0:k0 + wk],
                                                         start=True, stop=True),
                                 reads=[iqT, ikT], writes=[scp])
                            return scp
                        LA = 2
                        pend = [sc_mm(hh) for hh in range(min(LA, 16))]
                        for hh in range(16):
                            scp = pend.pop(0)
                            if hh + LA < 16:
                                pend.append(sc_mm(hh + LA))
                            r = rl.next()
                            S.op('act', lambda e: e.activation(out=r.t[:, 0:wk], in_=scp.t[:, 0:wk], func=AF.Relu),
                                 reads=[scp], writes=[r])
                            S.op('pe', lambda e: e.matmul(acc.t[:, 0:wk], Dg.t[:, hh, :], r.t[:, 0:wk], start=(hh == 0), stop=(hh == 15)),
                                 reads=[Dg, r], writes=[acc])
                        dcol = T + qb * 128
                        c_lo, c_hi = k0, k0 + wk
                        p_hi = min(c_hi, T)
                        if p_hi > c_lo:
                            S.op('dve', lambda e: e.tensor_scalar(out=score.t[:, c_lo:p_hi], in0=acc.t[:, 0:p_hi - c_lo], scalar1=flags.t[:, 1:2],
                                                                  scalar2=None, op0=ALU.add),
                                 reads=[acc, flags], writes=[score])
                        o_lo = max(c_lo, T)
                        o_hi = min(c_hi, dcol)
                        if o_hi > o_lo:
                            S.op('dve', lambda e: e.tensor_copy(out=score.t[:, o_lo:o_hi], in_=acc.t[:, o_lo - k0:o_hi - k0]),
                                 reads=[acc], writes=[score])
                        if c_lo <= dcol < c_hi:
                            S.op('dve', lambda e: e.tensor_tensor(out=score.t[:, dcol:dcol + 128], in0=acc.t[:, dcol - k0:dcol - k0 + 128],
                                                                  in1=inadm.t[:], op=ALU.add),
                                 reads=[acc, inadm], writes=[score])
                    def bis_iters(i0, i1):
                        for it in range(i0, i1):
                            S.op('dve', lambda e: e.tensor_scalar(out=junk.t[:, 0:Lk], in0=score.t[:, 0:Lk], scalar1=th.t[:], scalar2=None,
                                                                  op0=ALU.is_ge, op1=ALU.add, accum_out=cnt.t[:]),
                                 reads=[score, th], writes=[junk, cnt])
                            S.op('dve', lambda e: e.tensor_scalar(out=sg.t[:], in0=cnt.t[:], scalar1=KTOP - 0.5, scalar2=0.5, op0=ALU.is_ge, op1=ALU.subtract),
                                 reads=[cnt], writes=[sg])
                            S.op('dve', lambda e: e.scalar_tensor_tensor(out=th.t[:], in0=sg.t[:], scalar=steps.t[:, it:it + 1], in1=th.t[:],
                                                                         op0=ALU.mult, op1=ALU.add),
                                 reads=[sg, steps, th], writes=[th])

                    def bis_init():
                        S.op('dve', lambda e: e.tensor_reduce(out=mx.t[:], in_=score.t[:, 0:Lk], axis=AX.X, op=ALU.max),
                             reads=[score], writes=[mx])
                        S.op('dve', lambda e: e.tensor_scalar(out=junk.t[:, 0:T], in0=score.t[:, 0:T], scalar1=flags.t[:, 2:3], scalar2=None,
                                                              op0=ALU.add, op1=ALU.min, accum_out=mn.t[:]),
                             reads=[score, flags], writes=[junk, mn])
                        if qb > 0:
                            S.op('dve', lambda e: e.tensor_reduce(out=mn2.t[:], in_=score.t[:, T:T + qb * 128], axis=AX.X, op=ALU.min),
                                 reads=[score], writes=[mn2])
                            S.op('dve', lambda e: e.tensor_tensor(out=mn.t[:], in0=mn.t[:], in1=mn2.t[:], op=ALU.min),
                                 reads=[mn, mn2], writes=[mn])
                        S.op('dve', lambda e: e.tensor_tensor(out=mn.t[:], in0=mn.t[:], in1=mx.t[:], op=ALU.min),
                             reads=[mn, mx], writes=[mn])
                        S.op('dve', lambda e: e.tensor_tensor(out=rng.t[:], in0=mx.t[:], in1=mn.t[:], op=ALU.subtract),
                             reads=[mx, mn], writes=[rng])
                        S.op('dve', lambda e: e.scalar_tensor_tensor(out=th.t[:], in0=rng.t[:], scalar=0.5, in1=mn.t[:], op0=ALU.mult, op1=ALU.add),
                             reads=[rng, mn], writes=[th])
                        S.op('dve', lambda e: e.tensor_scalar(out=steps.t[:], in0=p2t.t[:], scalar1=rng.t[:], scalar2=None, op0=ALU.mult),
                             reads=[p2t, rng], writes=[steps])

                    def bis_final():
                        S.op('dve', lambda e: e.tensor_scalar(out=junk.t[:, 0:Lk], in0=score.t[:, 0:Lk], scalar1=-1.0e29, scalar2=None,
                                                              op0=ALU.is_gt, op1=ALU.add, accum_out=cnt.t[:]),
                             reads=[score], writes=[junk, cnt])
                        S.op('dve', lambda e: e.tensor_scalar(out=sg.t[:], in0=cnt.t[:], scalar1=KTOP + 0.5, scalar2=None, op0=ALU.is_ge),
                             reads=[cnt], writes=[sg])
                        S.op('dve', lambda e: e.tensor_scalar(out=mx.t[:], in0=sg.t[:], scalar1=1.0, scalar2=1.0e29, op0=ALU.subtract, op1=ALU.mult),
                             reads=[sg], writes=[mx])
                        S.op('dve', lambda e: e.scalar_tensor_tensor(out=th.t[:], in0=th.t[:], scalar=sg.t[:], in1=mx.t[:], op0=ALU.mult, op1=ALU.add),
                             reads=[th, sg, mx], writes=[th])
                        S.op('dve', lambda e: e.tensor_scalar(out=ng.t[:, 0:Lk], in0=score.t[:, 0:Lk], scalar1=th.t[:], scalar2=NEG,
                                                              op0=ALU.is_lt, op1=ALU.mult),
                             reads=[score, th], writes=[ng])

                    q = NIT // 4
                    return [lambda: (bis_init(), bis_iters(0, q)), lambda: bis_iters(q, 2 * q),
                            lambda: bis_iters(2 * q, 3 * q), lambda: (bis_iters(3 * q, NIT), bis_final())]

                stage_of = {}

                def att_phase(qb, chunks):
                    qsl = slice(qb * 128, (qb + 1) * 128)
                    nkt = NB + qb + 1
                    ng = negm_of.pop(qb)
                    qaT, sag = qaTs.next(), sags.next()
                    S.dma('sp', qaT.t[:], QA[:, :, qsl].rearrange('h p t -> p h t'), writes=[qaT])
                    S.dma('sp', sag.t[:], SAG[:, :, qsl].rearrange('h p t -> p h t'), writes=[sag])
                    if qb % 4 == 0:
                        stage_of['s'] = aast.next()
                    stage = stage_of['s']
                    q4 = qb % 4
                    for hg in range(4):
                        chunks[hg]()
                        units = list(range(nkt))

                        def logits(kt):
                            ksl = slice(kt * 128, (kt + 1) * 128)
                            lp = sl_ps.next()
                            S.op('pe', lambda e: e.matmul(lp.t[:], kaT.t[:, ksl], qaT.t[:, 4 * hg:4 * hg + 4, :].rearrange('p h t -> p (h t)'),
                                                         start=True, stop=False),
                                 reads=[kaT, qaT], writes=[lp])
                            S.op('pe', lambda e: e.matmul(lp.t[:], ng.t[:, ksl], ident.t[:], start=False, stop=True),
                                 reads=[ng, ident], writes=[lp])
                            return lp
                        LA = 2
                        pend = [logits(kt) for kt in units[:LA]]
                        for kt in units:
                            lp = pend.pop(0)
                            if kt + LA < nkt:
                                pend.append(logits(kt + LA))
                            pT = pTs.next()
                            S.op('act', lambda e: e.activation(out=pT.t[:], in_=lp.t[:], func=AF.Exp, scale=ATT_SCALE),
                                 reads=[lp], writes=[pT])
                            S.op('pe', lambda e: e.matmul(oo_ps[0].t[:], av.t[:, kt, :], pT.t[:], start=(kt == 0), stop=(kt == nkt - 1)),
                                 reads=[av, pT], writes=[oo_ps[0]])
                            S.op('pe', lambda e: e.matmul(dd_ps[0].t[:], ones_b.t[:], pT.t[:], start=(kt == 0), stop=(kt == nkt - 1)),
                                 reads=[ones_b, pT], writes=[dd_ps[0]])
                        S.op('dve', lambda e: e.reciprocal(out=rd.t[:], in_=dd_ps[0].t[:]), reads=[dd_ps[0]], writes=[rd])
                        S.op('dve', lambda e: e.tensor_tensor(out=tq.t[:], in0=oo_ps[0].t[:], in1=rd.t[:], op=ALU.mult),
                             reads=[oo_ps[0], rd], writes=[tq])
                        S.op('pool', lambda e: e.tensor_tensor(out=stage.t[:, 4 * hg:4 * hg + 4, q4 * 128:(q4 + 1) * 128],
                                                               in0=tq.t[:].rearrange('p (h t) -> p h t', h=4),
                                                               in1=sag.t[:, 4 * hg:4 * hg + 4, :], op=ALU.mult),
                             reads=[tq, sag], writes=[stage])
                    if q4 == 3:
                        t0 = (qb // 4) * 512
                        S.dma('sp', AAT[:, :, t0:t0 + 512].rearrange('c p t -> p c t'), stage.t[:], reads=[stage])

                for f in idx_phase(0):
                    f()
                for qb in range(NB):
                    chunks = idx_phase(qb + 1) if qb + 1 < NB else [lambda: None] * 4
                    att_phase(qb, chunks)
            S.barrier()
            with contextlib.ExitStack() as p5:
                bufA = sb(p5, 'bufA', [128, KC, T], BF16)
                bufM = sb(p5, 'bufM', [128, KC, T], BF16)
                gts = Rot([sb(p5, 'gt%d' % i, [128, T], BF16) for i in range(2)])
                tm5 = Rot([sb(p5, 't5%d' % i, [128, 512], F32) for i in range(2)])
                xts = Rot([sb(p5, 'x5%d' % i, [128, 512], F32) for i in range(3)])
                xos = Rot([sb(p5, 'xo%d' % i, [128, 512], F32) for i in range(3)])
                pb5 = Rot([ps(p5, 'p5%d' % i, [128, 512]) for i in range(4)])
                for br, (Asrc, wsrc, Gsrc) in enumerate(((ART, w_br[l], SGA), (AAT, w_ba[l], SGB))):
                    S.dma('sp', bufA.t[:], Asrc.rearrange('c p t -> p c t'), writes=[bufA])
                    for wp in range(8):
                        wt = load_w(wsrc, wp * 256)
                        for mh in range(2):
                            mt = wp * 2 + mh
                            gt = gts.next()
                            S.dma('sp', gt.t[:], Gsrc[mt], writes=[gt])
                            for tt in range(NT):
                                tsl = slice(tt * 512, (tt + 1) * 512)
                                pbuf = pb5.next()
                                for kc in range(KC):
                                    S.op('pe', lambda e: e.matmul(pbuf.t[:], wt.t[:, kc, mh * 128:(mh + 1) * 128], bufA.t[:, kc, tsl],
                                                                 start=(kc == 0), stop=(kc == KC - 1)),
                                         reads=[wt, bufA], writes=[pbuf])
                                if br == 0:
                                    S.op('dve', lambda e: e.tensor_tensor(out=bufM.t[:, mt, tsl], in0=pbuf.t[:], in1=gt.t[:, tsl], op=ALU.mult),
                                         reads=[pbuf, gt], writes=[bufM])
                                else:
                                    t5 = tm5.next()
                                    S.op('dve', lambda e: e.tensor_tensor(out=t5.t[:], in0=pbuf.t[:], in1=gt.t[:, tsl], op=ALU.mult),
                                         reads=[pbuf, gt], writes=[t5])
                                    S.op('pool', lambda e: e.tensor_tensor(out=bufM.t[:, mt, tsl], in0=t5.t[:], in1=bufM.t[:, mt, tsl], op=ALU.add),
                                         reads=[t5, bufM], writes=[bufM])
                for wp in range(8):
                    wt = load_w(w_o[l], wp * 256)
                    for mh in range(2):
                        mt = wp * 2 + mh
                        for tt in range(NT):
                            tsl = slice(tt * 512, (tt + 1) * 512)
                            xt5 = xts.next()
                            S.dma('sp', xt5.t[:], x_src[mt * 128:(mt + 1) * 128, tsl], writes=[xt5])
                            pbuf = pb5.next()
                            for kc in range(KC):
                                S.op('pe', lambda e: e.matmul(pbuf.t[:], wt.t[:, kc, mh * 128:(mh + 1) * 128], bufM.t[:, kc, tsl],
                                                             start=(kc == 0), stop=(kc == KC - 1)),
                                     reads=[wt, bufM], writes=[pbuf])
                            xo = xos.next()
                            S.op('dve', lambda e: e.tensor_tensor(out=xo.t[:], in0=pbuf.t[:], in1=xt5.t[:], op=ALU.add),
                                 reads=[pbuf, xt5], writes=[xo])
                            S.dma('sp', x_dst[mt * 128:(mt + 1) * 128, tsl], xo.t[:], reads=[xo])
            S.barrier()
    return nc


def _consts(TF, L, norm_g, ret_out_g, att_q_g, att_k_g, idx_k_g, rank):
    f32 = np.float32
    T = TF // 2
    pos = np.arange(rank * T, (rank + 1) * T).astype(f32)
    inv = (1.0 / (np.float32(10000.0) ** np.linspace(0.0, 1.0, 128, dtype=f32))).astype(f32)
    ang = (pos[:, None] * inv[None, :]).astype(f32)
    cs = np.stack([np.cos(ang).T, np.sin(ang).T]).astype(f32)
    gam = 1.0 - 2.0 ** (-5.0 - np.arange(8, dtype=np.float64))
    i = np.arange(128)
    ii, jj = i[:, None], i[None, :]
    same = (ii // 64) == (jj // 64)
    later = (ii // 64) > (jj // 64)
    rmask = np.zeros((128, 8, 128), np.float64)
    for h in range(8):
        m = np.where(same, gam[h] ** np.abs(ii - jj), np.where(later, gam[h] ** (ii - jj), 0.0))
        rmask[:, h, :] = m.T / 16.0
    dq = np.zeros((128, 8, 256), np.float64)
    for h in range(8):
        dq[:, h, :] = np.tile(gam[h] ** (i + 1.0), 2)[None, :]
    dk = np.stack([gam[h] ** (127.0 - i) / 16.0 for h in range(8)], axis=1)
    nb = T // 128
    dkp = np.zeros((128, 8, nb), np.float64)
    for h in range(8):
        for b in range(nb):
            dkp[:, h, b] = gam[h] ** (T - 1.0 - (b * 128 + i)) / 16.0
    flags = np.zeros((128, 4), np.float64)
    flags[:, 0] = float(rank)
    flags[:, 1] = 0.0 if rank == 1 else -1.0e30
    flags[:, 2] = 0.0 if rank == 1 else 2.0e30
    ident = np.tile(np.eye(128, dtype=f32), (1, 4))
    inadm = np.where((jj >= 64) & (ii < 64), -1.0e30, 0.0).astype(f32)
    p2tab = np.tile((2.0 ** -(np.arange(NIT) + 1.0))[None, :], (128, 1)).astype(f32)
    gcol = norm_g.reshape(L, 16, 128).transpose(2, 0, 1).reshape(128, L * 16)
    gret = np.broadcast_to(ret_out_g.reshape(1, L * 2048), (128, L * 2048))
    gqk = np.stack([att_q_g, att_k_g, np.concatenate([idx_k_g, idx_k_g], axis=1)], axis=1)
    gqk = gqk.transpose(2, 0, 1).reshape(128, L * 3)
    c = lambda a: np.ascontiguousarray(a, dtype=f32)
    return {'cs': c(cs), 'rmask': c(rmask.reshape(128, 1024)), 'dq': c(dq.reshape(128, 2048)), 'dk': c(dk),
            'dkp': c(dkp.reshape(128, 8 * nb)), 'flags': c(flags),
            'ident': c(ident), 'inadm': c(inadm), 'p2tab': c(p2tab), 'gcol': c(gcol), 'gret': c(gret), 'gqk': c(gqk)}


def run_layers(x_list, layers, norm_g, w_in, ret_out_g, att_q_g, att_k_g, idx_k_g, w_branch_ret, w_branch_att, w_out):
    L = len(layers)
    TF = x_list[0].shape[0]
    T = TF // 2
    sl = np.array(layers)
    shared = {'w_in': np.ascontiguousarray(w_in[sl]), 'w_br': np.ascontiguousarray(w_branch_ret[sl]),
              'w_ba': np.ascontiguousarray(w_branch_att[sl]), 'w_o': np.ascontiguousarray(w_out[sl])}
    consts = [_consts(TF, L, norm_g[sl], ret_out_g[sl], att_q_g[sl], att_k_g[sl], idx_k_g[sl], r) for r in range(2)]
    nc = build(L, TF)
    in_maps = []
    for x in x_list:
        for r in range(2):
            m = dict(shared)
            m.update(consts[r])
            m['xT'] = np.ascontiguousarray(x[r * T:(r + 1) * T].T, dtype=np.float32)
            in_maps.append(m)
    res = run_bass_kernel_spmd(nc, in_maps, core_ids=list(range(len(in_maps))))
    outs = []
    for i in range(len(x_list)):
        outs.append(np.concatenate([res.results[2 * i]['outT'].T, res.results[2 * i + 1]['outT'].T], axis=0))
    return outs


def kernel(x, norm_g, w_in, ret_out_g, att_q_g, att_k_g, idx_k_g, w_branch_ret, w_branch_att, w_out):
    x = np.asarray(x, dtype=np.float32)
    B = x.shape[0]
    depth = w_in.shape[0]
    args = [np.asarray(a, dtype=np.float32) for a in
            (norm_g, w_in, ret_out_g, att_q_g, att_k_g, idx_k_g, w_branch_ret, w_branch_att, w_out)]
    outs = run_layers([x[b] for b in range(B)], list(range(depth)), *args)
    return np.stack(outs, axis=0).astype(np.float32)
```

```python
import contextlib
import numpy as np
import concourse.bass as bass
import concourse.mybir as mybir
from concourse.bass_utils import run_bass_kernel_spmd

F32 = mybir.dt.float32
BF16 = mybir.dt.bfloat16
AF = mybir.ActivationFunctionType
ALU = mybir.AluOpType
AX = mybir.AxisListType

D = 2048
KC = 16
N_IN = 17744
O_RQ, O_RK, O_RV, O_RG, O_AQ, O_AK, O_AV, O_AG, O_IQ, O_IK, O_IW, O_GA, O_GB = (
    0, 2048, 4096, 6144, 8192, 10240, 10368, 10496, 12544, 13568, 13632, 13648, 15696)
EPS = 1e-6
NIT = 20
NEG = -30000.0
ATT_SCALE = 128 ** -0.5


class Buf:
    def __init__(s, name, t):
        s.name = name
        s.t = t
        s.w = None
        s.r = {}


class Sched:
    def __init__(s, nc, stack):
        s.nc = nc
        s.stack = stack
        s.engs = {'pe': nc.tensor, 'act': nc.scalar, 'dve': nc.vector, 'pool': nc.gpsimd, 'sp': nc.sync}
        s.sem = {k: stack.enter_context(nc.semaphore('s_' + k)) for k in s.engs}
        s.cnt = {k: 0 for k in s.engs}
        s.waited = {k: {} for k in s.engs}

    def dsem(s, key):
        k = 'dma:' + key
        if k not in s.sem:
            s.sem[k] = s.stack.enter_context(s.nc.semaphore('d_' + key))
            s.cnt[k] = 0
        return k

    def wait(s, eng, tok):
        k, v = tok
        if s.waited[eng].get(k, 0) >= v:
            return
        s.engs[eng].wait_ge(s.sem[k], v)
        s.waited[eng][k] = v

    def _deps(s, eng, reads, writes):
        for b in reads:
            if b.w is not None:
                if b.w[0] == eng and eng == 'pe':
                    continue
                s.wait(eng, b.w)
        for b in writes:
            if b.w is not None and b.w[0] != eng:
                s.wait(eng, b.w)
            for k, v in b.r.items():
                if k != eng:
                    s.wait(eng, (k, v))

    def _mark(s, tok, reads, writes):
        for b in reads:
            b.r[tok[0]] = tok[1]
        for b in writes:
            b.w = tok
            b.r = {}

    def op(s, eng, fn, reads=(), writes=()):
        s._deps(eng, reads, writes)
        ins = fn(s.engs[eng])
        s.cnt[eng] += 1
        ins.then_inc(s.sem[eng], 1)
        s._mark((eng, s.cnt[eng]), reads, writes)

    def dma(s, q, out, in_, reads=(), writes=()):
        s._deps(q, reads, writes)
        key = (list(writes) + list(reads))[0].name
        k = s.dsem(key)
        ins = s.engs[q].dma_start(out=out, in_=in_)
        s.cnt[k] += 16
        ins.then_inc(s.sem[k], 16)
        s._mark((k, s.cnt[k]), reads, writes)

    def barrier(s):
        for k in list(s.sem):
            if k != 'sp' and s.cnt[k] > 0:
                s.wait('sp', (k, s.cnt[k]))
        s.cnt['sp'] += 1
        s.engs['sp'].sem_inc(s.sem['sp'], 1)
        for e in s.engs:
            if e != 'sp':
                s.wait(e, ('sp', s.cnt['sp']))
        for e in s.engs:
            for k in s.sem:
                s.waited[e][k] = s.cnt[k]


class Rot:
    def __init__(s, bufs):
        s.bufs = bufs
        s.i = 0

    def next(s):
        b = s.bufs[s.i % len(s.bufs)]
        s.i += 1
        return b


def build(L, TF, debug=False):
    T = TF // 2
    NT = T // 512
    NB = T // 128
    KTOP = min(256, TF // 4)
    nc = bass.Bass('TRN2', target_bir_lowering=False, num_devices=8)
    skind = 'ExternalOutput' if debug else 'Internal'

    def din(name, shape, dt=F32):
        return nc.dram_tensor(name, shape, dt, kind='ExternalInput').ap()

    def dscr(name, shape, dt=BF16):
        return nc.dram_tensor(name, shape, dt, kind=skind).ap()

    xT_in = din('xT', [D, T])
    w_in = din('w_in', [L, D, N_IN])
    w_br = din('w_br', [L, D, D])
    w_ba = din('w_ba', [L, D, D])
    w_o = din('w_o', [L, D, D])
    cs_d = din('cs', [2, 128, T])
    rmask_d = din('rmask', [128, 8 * 128])
    dq_d = din('dq', [128, 8 * 256])
    dk_d = din('dk', [128, 8])
    ident_d = din('ident', [128, 512])
    inadm_d = din('inadm', [128, 128])
    p2_d = din('p2tab', [128, NIT])
    gcol_d = din('gcol', [128, L * 16])
    gret_d = din('gret', [128, L * 2048])
    gqk_d = din('gqk', [128, L * 3])
    flags_d = din('flags', [128, 4])
    dkp_d = din('dkp', [128, 8 * NB])
    outT = nc.dram_tensor('outT', [D, T], F32, kind='ExternalOutput').ap()

    xT_s = dscr('xT_s', [D, T], F32)
    RQ = dscr('RQ', [8, 2, 128, T])
    SRG = dscr('SRG', [T, 2048])
    QA = dscr('QA', [16, 128, T])
    SAG = dscr('SAG', [16, 128, T])
    IQ = dscr('IQ', [8, 128, T])
    IW = dscr('IW', [T, 16], F32)
    SGA = dscr('SGA', [16, 128, T])
    SGB = dscr('SGB', [16, 128, T])
    ART = dscr('ART', [16, 128, T])
    AAT = dscr('AAT', [16, 128, T])
    NXR = 4480
    XO = dscr('XO', [NXR, T])
    XCH = [(0, 1024), (1024, 2048), (2048, 3072), (3072, 4096), (4096, 4480)]
    XGs = [dscr('XG%d' % k, [2 * (b - a), T]) for k, (a, b) in enumerate(XCH)]
    RK = XO[0:2048].rearrange('(h c p) t -> h c p t', h=8, c=2)
    RV = XO[2048:4096].rearrange('(t e) c -> t (e c)', e=2048 // T)
    KA, IK = XO[4096:4224], XO[4224:4352]
    AV = XO[4352:4480].rearrange('r (e c) -> (r e) c', c=128)
    RKp_ = [XGs[k][0:1024].rearrange('(h c p) t -> h c p t', h=4, c=2) for k in (0, 1)]
    RKp = [RKp_[h // 4][h % 4] for h in range(8)]
    RVp = [XGs[k][0:1024].rearrange('(t e) c -> t (e c)', e=2048 // T) for k in (2, 3)]
    KAp, IKp = XGs[4][0:128], XGs[4][128:256]
    AVp = XGs[4][256:384].rearrange('r (e c) -> (r e) c', c=128)

    gam = [1.0 - 2.0 ** (-5.0 - h) for h in range(8)]

    with contextlib.ExitStack() as st:
        S = Sched(nc, st)

        uid = [0]

        def sb(stk, name, shape, dt):
            uid[0] += 1
            return Buf(name, stk.enter_context(nc.sbuf_tensor('%s_u%d' % (name, uid[0]), shape, dt)))

        def ps(stk, name, shape, dt=F32):
            uid[0] += 1
            return Buf(name, stk.enter_context(nc.psum_tensor('%s_u%d' % (name, uid[0]), shape, dt)))

        ident = sb(st, 'ident', [128, 512], BF16)
        S.dma('pool', ident.t[:], ident_d, writes=[ident])
        ones_f = sb(st, 'ones_f', [128, 128], F32)
        S.op('dve', lambda e: e.memset(ones_f.t[:], 1.0), writes=[ones_f])
        ones_b = sb(st, 'ones_b', [128, 128], BF16)
        S.op('dve', lambda e: e.memset(ones_b.t[:], 1.0), writes=[ones_b])
        mhalf = sb(st, 'mhalf', [128, 1], F32)
        S.op('dve', lambda e: e.memset(mhalf.t[:], -0.5), writes=[mhalf])
        gcol = sb(st, 'gcol', [128, L * 16], F32)
        S.dma('sp', gcol.t[:], gcol_d, writes=[gcol])
        gqk = sb(st, 'gqk', [128, L * 3], F32)
        S.dma('sp', gqk.t[:], gqk_d, writes=[gqk])
        flags = sb(st, 'flags', [128, 4], F32)
        S.dma('sp', flags.t[:], flags_d, writes=[flags])
        csem = st.enter_context(nc.semaphore('csem'))
        ccn = [0]
        wts = Rot([sb(st, 'wt%d' % i, [128, KC, 256], BF16) for i in range(3)])

        wq = {'specs': [], 'bufs': {}, 'issued': 0, 'next': 0}

        def _issue_w(upto):
            while wq['issued'] < min(upto, len(wq['specs'])):
                i = wq['issued']
                wt = wts.next()
                for (src, c0, ncol, dst0) in wq['specs'][i]:
                    S.dma('pool', wt.t[:, :, dst0:dst0 + ncol],
                          src[:, c0:c0 + ncol].rearrange('(kc p) m -> p kc m', p=128), writes=[wt])
                wq['bufs'][i] = wt
                wq['issued'] += 1

        def next_w():
            i = wq['next']
            _issue_w(i + 3)
            wq['next'] += 1
            return wq['bufs'].pop(i)

        def layer_wspecs(l):
            win = w_in[l]
            sp = []
            for off in (O_RQ, O_RK, O_RV, O_RG, O_AQ):
                for h in range(8):
                    sp.append([(win, off + h * 256, 256, 0)])
            sp.append([(win, O_AK, 256, 0)])
            for h in range(8):
                sp.append([(win, O_AG + h * 256, 256, 0)])
            for h in range(4):
                sp.append([(win, O_IQ + h * 256, 256, 0)])
            sp.append([(win, O_IK, 64, 0), (win, O_IK, 64, 64), (win, O_IW, 16, 128)])
            for off in (O_GA, O_GB):
                for h in range(8):
                    sp.append([(win, off + h * 256, 256, 0)])
            for wsrc in (w_br[l], w_ba[l], w_o[l]):
                for h in range(8):
                    sp.append([(wsrc, h * 256, 256, 0)])
            return sp

        def load_w(src=None, c0=None, *a, **k):
            if c0 is not None:
                assert wq['specs'][wq['next']][0][1] == c0, (wq['next'], c0)
            return next_w()

        for l in range(L):
            x_src = xT_in if l == 0 else xT_s
            x_dst = outT if l == L - 1 else xT_s
            wq['specs'] = layer_wspecs(l)
            wq['bufs'] = {}
            wq['issued'] = 0
            wq['next'] = 0
            win = w_in[l]
            with contextlib.ExitStack() as p12:
                hT = sb(p12, 'hT', [128, KC, T], BF16)
                with contextlib.ExitStack() as p1:
                    xtr = Rot([sb(p1, 'xt%d' % i, [128, KC, 512], F32) for i in range(2)])
                    sqr = Rot([sb(p1, 'sq%d' % i, [128, 512], F32) for i in range(3)])
                    rs = sb(p1, 'rs', [128, 512], F32)
                    rstd = sb(p1, 'rstd', [128, 512], F32)
                    ssp = ps(p1, 'ssp', [128, 512])
                    for tt in range(NT):
                        tsl = slice(tt * 512, (tt + 1) * 512)
                        xt = xtr.next()
                        S.dma('sp', xt.t[:], x_src[:, tsl].rearrange('(kc p) t -> p kc t', p=128), writes=[xt])
                        for kc in range(KC):
                            sq = sqr.next()
                            S.op('act', lambda e: e.activation(out=sq.t[:], in_=xt.t[:, kc, :], func=AF.Square),
                                 reads=[xt], writes=[sq])
                            S.op('pe', lambda e: e.matmul(ssp.t[:], ones_f.t[:], sq.t[:], start=(kc == 0),
                                                         stop=(kc == KC - 1)),
                                 reads=[ones_f, sq], writes=[ssp])
                        S.op('act', lambda e: e.activation(out=rs.t[:], in_=ssp.t[:], func=AF.Sqrt,
                                                           scale=1.0 / D, bias=EPS),
                             reads=[ssp], writes=[rs])
                        S.op('dve', lambda e: e.reciprocal(out=rstd.t[:], in_=rs.t[:]), reads=[rs], writes=[rstd])
                        for kc in range(KC):
                            S.op('dve', lambda e: e.scalar_tensor_tensor(
                                out=hT.t[:, kc, tsl], in0=xt.t[:, kc, :], scalar=gcol.t[:, l * 16 + kc:l * 16 + kc + 1],
                                in1=rstd.t[:], op0=ALU.mult, op1=ALU.mult), reads=[xt, gcol, rstd], writes=[hT])
                S.barrier()
                with contextlib.ExitStack() as p2:
                    cs = sb(p2, 'cs', [128, 2, T], F32)
                    S.dma('sp', cs.t[:], cs_d.rearrange('c p t -> p c t'), writes=[cs])
                    pb = Rot([ps(p2, 'pb%d' % i, [128, 512]) for i in range(6)])
                    ssq = ps(p2, 'ssq', [128, 512])
                    tmpf = Rot([sb(p2, 'tf%d' % i, [128, 512], F32) for i in range(8)])
                    outb = Rot([sb(p2, 'ob%d' % i, [128, 512], BF16) for i in range(6)])
                    outf = Rot([sb(p2, 'of%d' % i, [128, 16], F32) for i in range(2)])

                    def fm_tile(wt, mh, tt, pbuf):
                        tsl = slice(tt * 512, (tt + 1) * 512)
                        for kc in range(KC):
                            S.op('pe', lambda e: e.matmul(pbuf.t[:], wt.t[:, kc, mh * 128:(mh + 1) * 128],
                                                         hT.t[:, kc, tsl], start=(kc == 0), stop=(kc == KC - 1)),
                                 reads=[wt, hT], writes=[pbuf])

                    def tm_tile(wt, c0, ncol, tb, pbuf):
                        for kc in range(KC):
                            S.op('pe', lambda e: e.matmul(pbuf.t[:, 0:ncol], hT.t[:, kc, tb * 128:(tb + 1) * 128],
                                                         wt.t[:, kc, c0:c0 + ncol], start=(kc == 0),
                                                         stop=(kc == KC - 1)),
                                 reads=[wt, hT], writes=[pbuf])

                    def store(dst, ob, src_ap=None):
                        S.dma('sp', dst, ob.t[:] if src_ap is None else src_ap, reads=[ob])

                    def fm_act(wt, mh, func, dst_fn):
                        for tt in range(NT):
                            pbuf = pb.next()
                            fm_tile(wt, mh, tt, pbuf)
                            ob = outb.next()
                            S.op('act', lambda e: e.activation(out=ob.t[:], in_=pbuf.t[:], func=func),
                                 reads=[pbuf], writes=[ob])
                            store(dst_fn(tt), ob)

                    def fm_norm(wt, mh, gidx, nmean, dst_fn):
                        for tt in range(NT):
                            pbuf = pb.next()
                            fm_tile(wt, mh, tt, pbuf)
                            sq = tmpf.next()
                            S.op('act', lambda e: e.activation(out=sq.t[:], in_=pbuf.t[:], func=AF.Square),
                                 reads=[pbuf], writes=[sq])
                            S.op('pe', lambda e: e.matmul(ssq.t[:], ones_f.t[:], sq.t[:], start=True, stop=True),
                                 reads=[ones_f, sq], writes=[ssq])
                            r = tmpf.next()
                            S.op('act', lambda e: e.activation(out=r.t[:], in_=ssq.t[:], func=AF.Sqrt,
                                                               scale=1.0 / nmean, bias=EPS),
                                 reads=[ssq], writes=[r])
                            ri = tmpf.next()
                            S.op('dve', lambda e: e.reciprocal(out=ri.t[:], in_=r.t[:]), reads=[r], writes=[ri])
                            ob = outb.next()
                            S.op('dve', lambda e: e.scalar_tensor_tensor(
                                out=ob.t[:], in0=pbuf.t[:], scalar=gqk.t[:, l * 3 + gidx:l * 3 + gidx + 1],
                                in1=ri.t[:], op0=ALU.mult, op1=ALU.mult), reads=[pbuf, gqk, ri], writes=[ob])
                            store(dst_fn(tt), ob)

                    def tm_act(wt, c0, ncol, func, dst_fn):
                        for tb in range(NB):
                            pbuf = pb.next()
                            tm_tile(wt, c0, ncol, tb, pbuf)
                            ob = outb.next()
                            S.op('act', lambda e: e.activation(out=ob.t[:, 0:ncol], in_=pbuf.t[:, 0:ncol], func=func),
                                 reads=[pbuf], writes=[ob])
                            store(dst_fn(tb), ob, ob.t[:, 0:ncol])

                    for (off, dst) in ((O_RQ, RQ), (O_RK, RK)):
                        for h in range(8):
                            wt = load_w(win, off + h * 256)
                            for tt in range(NT):
                                tsl = slice(tt * 512, (tt + 1) * 512)
                                p1b = pb.next()
                                fm_tile(wt, 0, tt, p1b)
                                p2b = pb.next()
                                fm_tile(wt, 1, tt, p2b)
                                a, b, c, d = tmpf.next(), tmpf.next(), tmpf.next(), tmpf.next()
                                S.op('dve', lambda e: e.tensor_tensor(out=a.t[:], in0=p1b.t[:], in1=cs.t[:, 0, tsl], op=ALU.mult),
                                     reads=[p1b, cs], writes=[a])
                                S.op('dve', lambda e: e.tensor_tensor(out=b.t[:], in0=p2b.t[:], in1=cs.t[:, 1, tsl], op=ALU.mult),
                                     reads=[p2b, cs], writes=[b])
                                S.op('dve', lambda e: e.tensor_tensor(out=c.t[:], in0=p2b.t[:], in1=cs.t[:, 0, tsl], op=ALU.mult),
                                     reads=[p2b, cs], writes=[c])
                                S.op('dve', lambda e: e.tensor_tensor(out=d.t[:], in0=p1b.t[:], in1=cs.t[:, 1, tsl], op=ALU.mult),
                                     reads=[p1b, cs], writes=[d])
                                o1, o2 = outb.next(), outb.next()
                                S.op('pool', lambda e: e.tensor_tensor(out=o1.t[:], in0=a.t[:], in1=b.t[:], op=ALU.subtract),
                                     reads=[a, b], writes=[o1])
                                S.op('pool', lambda e: e.tensor_tensor(out=o2.t[:], in0=c.t[:], in1=d.t[:], op=ALU.add),
                                     reads=[c, d], writes=[o2])
                                store(dst[h, 0, :, tsl], o1)
                                store(dst[h, 1, :, tsl], o2)
                    for (off, dst, func) in ((O_RV, RV, AF.Copy), (O_RG, SRG, AF.Silu)):
                        for h in range(8):
                            wt = load_w(win, off + h * 256)
                            tm_act(wt, 0, 256, func, lambda tb: dst[tb * 128:(tb + 1) * 128, h * 256:(h + 1) * 256])
                    for hp in range(8):
                        wt = load_w(win, O_AQ + hp * 256)
                        for mh in range(2):
                            hh = hp * 2 + mh
                            fm_norm(wt, mh, 0, 128.0, lambda tt: QA[hh, :, tt * 512:(tt + 1) * 512])
                    wt = load_w(win, O_AK)
                    fm_norm(wt, 0, 1, 128.0, lambda tt: KA[:, tt * 512:(tt + 1) * 512])
                    tm_act(wt, 128, 128, AF.Copy, lambda tb: AV[tb * 128:(tb + 1) * 128, :])
                    for hp in range(8):
                        wt = load_w(win, O_AG + hp * 256)
                        for mh in range(2):
                            hh = hp * 2 + mh
                            fm_act(wt, mh, AF.Silu, lambda tt: SAG[hh, :, tt * 512:(tt + 1) * 512])
                    for pp in range(4):
                        wt = load_w(win, O_IQ + pp * 256)
                        for mh in range(2):
                            pr = pp * 2 + mh
                            fm_act(wt, mh, AF.Copy, lambda tt: IQ[pr, :, tt * 512:(tt + 1) * 512])
                    wt = load_w()
                    fm_norm(wt, 0, 2, 128.0, lambda tt: IK[:, tt * 512:(tt + 1) * 512])
                    for tb in range(NB):
                        pbuf = pb.next()
                        tm_tile(wt, 128, 16, tb, pbuf)
                        of = outf.next()
                        S.op('act', lambda e: e.activation(out=of.t[:], in_=pbuf.t[:, 0:16], func=AF.Copy),
                             reads=[pbuf], writes=[of])
                        S.dma('sp', IW[tb * 128:(tb + 1) * 128, :], of.t[:], reads=[of])
                    S.barrier()
                    for k, (ra, rb) in enumerate(XCH):
                        cc = nc.gpsimd.collective_compute('AllGather', ALU.bypass, replica_groups=[[0, 1], [2, 3], [4, 5], [6, 7]],
                                                          ins=[XO[ra:rb]], outs=[XGs[k]])
                        ccn[0] += 1
                        cc.then_inc(csem, 1)
                    for (off, dst) in ((O_GA, SGA), (O_GB, SGB)):
                        for hp in range(8):
                            wt = load_w(win, off + hp * 256)
                            for mh in range(2):
                                cc = hp * 2 + mh
                                fm_act(wt, mh, AF.Sigmoid, lambda tt: dst[cc, :, tt * 512:(tt + 1) * 512])
            S.barrier()
            nc.gpsimd.wait_ge(csem, ccn[0])
            S.cnt['pool'] += 1
            nc.gpsimd.sem_inc(S.sem['pool'], 1)
            S.barrier()
            with contextlib.ExitStack() as p3:
                rmask = sb(p3, 'rmask', [128, 8, 128], F32)
                S.dma('sp', rmask.t[:], rmask_d.rearrange('p (h i) -> p h i', h=8), writes=[rmask])
                dq = sb(p3, 'dq', [128, 8, 256], BF16)
                S.dma('pool', dq.t[:], dq_d.rearrange('p (h i) -> p h i', h=8), writes=[dq])
                dk = sb(p3, 'dk', [128, 8], F32)
                S.dma('sp', dk.t[:], dk_d, writes=[dk])
                gret = sb(p3, 'gret', [128, 2048], F32)
                S.dma('sp', gret.t[:], gret_d[:, l * 2048:(l + 1) * 2048], writes=[gret])
                qTs = Rot([sb(p3, 'qT%d' % i, [128, 2, T], BF16) for i in range(2)])
                kTs = Rot([sb(p3, 'kT%d' % i, [128, 2, T], BF16) for i in range(2)])
                vs = Rot([sb(p3, 'v%d' % i, [128, NB, 256], BF16) for i in range(2)])
                srgs = Rot([sb(p3, 'srg%d' % i, [128, NB, 256], BF16) for i in range(2)])
                kTps = Rot([sb(p3, 'kTp%d' % i, [128, 2, T], BF16) for i in range(2)])
                vps = Rot([sb(p3, 'vp%d' % i, [128, NB, 256], BF16) for i in range(2)])
                dkp = sb(p3, 'dkp', [128, 8 * NB], F32)
                S.dma('sp', dkp.t[:], dkp_d, writes=[dkp])
                sTs = Rot([sb(p3, 'sT%d' % i, [128, 128], BF16) for i in range(2)])
                qds = Rot([sb(p3, 'qd%d' % i, [128, 256], BF16) for i in range(2)])
                kds = Rot([sb(p3, 'kd%d' % i, [128, 256], BF16) for i in range(2)])
                S32s = Rot([sb(p3, 'S32_%d' % i, [128, 512], F32) for i in range(2)])
                kdps = Rot([sb(p3, 'kdp%d' % i, [128, 256], BF16) for i in range(2)])
                Sbs = Rot([sb(p3, 'Sb%d' % i, [128, 2, 256], BF16) for i in range(4)])
                osbs = Rot([sb(p3, 'osb%d' % i, [128, 256], F32) for i in range(2)])
                onbs = Rot([sb(p3, 'onb%d' % i, [128, 256], F32) for i in range(2)])
                abs_ = Rot([sb(p3, 'ab%d' % i, [128, 256], BF16) for i in range(2)])
                sss = Rot([sb(p3, 'ss%d' % i, [128, 1], F32) for i in range(2)])
                s2s = Rot([sb(p3, 's2%d' % i, [128, 1], F32) for i in range(2)])
                rsds = Rot([sb(p3, 'rsd%d' % i, [128, 1], F32) for i in range(2)])
                aTst = Rot([sb(p3, 'aTst%d' % i, [128, 2, 512], BF16) for i in range(2)])
                sT_ps = Rot([ps(p3, 'sTp0', [128, 512])])
                kd_ps = Rot([ps(p3, 'kdp_0', [128, 1024], BF16)])
                o_ps = Rot([ps(p3, 'op%d' % i, [128, 512]) for i in range(2)])
                aT_ps = ps(p3, 'aTp', [128, 1024], BF16)
                S_ps = ps(p3, 'Sp', [128, 512])
                pa_ps = ps(p3, 'pa', [128, 512])
                kdp_ps = ps(p3, 'kdpp', [128, 1024], BF16)

                def make_prefix(hh):
                    kTp, vp = kTps.next(), vps.next()
                    S.dma('sp', kTp.t[:], RKp[hh].rearrange('c p t -> p c t'), writes=[kTp])
                    for hf in range(2):
                        S.dma('sp', vp.t[:, hf * (NB // 2):(hf + 1) * (NB // 2), :],
                              RVp[hf][:, hh * 256:(hh + 1) * 256].rearrange('(b p) c -> p b c', p=128), writes=[vp])
                    S32h = S32s.next()
                    Sbh = Sbs.next()
                    res = {'S32': S32h, 'Sb0': Sbh}

                    def step(blk):
                        bsl = slice(blk * 128, (blk + 1) * 128)
                        for c in range(2):
                            S.op('pe', lambda e: e.transpose(kdp_ps.t[:, c * 128:(c + 1) * 128], kTp.t[:, c, bsl], ident.t[:, 0:128]),
                                 reads=[kTp, ident], writes=[kdp_ps])
                        kdp = kdps.next()
                        S.op('act', lambda e: e.activation(out=kdp.t[:], in_=kdp_ps.t[:, 0:256], func=AF.Identity,
                                                           scale=dkp.t[:, hh * NB + blk:hh * NB + blk + 1]),
                             reads=[kdp_ps, dkp], writes=[kdp])
                        for c in range(2):
                            S.op('pe', lambda e: e.matmul(pa_ps.t[:, c * 256:(c + 1) * 256], kdp.t[:, c * 128:(c + 1) * 128], vp.t[:, blk, :],
                                                         start=(blk == 0 and c == 0), stop=(blk == NB - 1), skip_group_check=True),
                                 reads=[kdp, vp], writes=[pa_ps])

                    def final():
                        for c in range(2):
                            S.op('dve', lambda e: e.tensor_scalar(out=S32h.t[:, c * 256:(c + 1) * 256], in0=pa_ps.t[:, c * 256:(c + 1) * 256],
                                                                  scalar1=flags.t[:, 0:1], scalar2=None, op0=ALU.mult),
                                 reads=[pa_ps, flags], writes=[S32h])
                        S.op('act', lambda e: e.activation(out=Sbh.t[:].rearrange('p c v -> p (c v)'), in_=S32h.t[:], func=AF.Copy),
                             reads=[S32h], writes=[Sbh])
                    res['steps'] = [(lambda blk=blk: step(blk)) for blk in range(NB)]
                    res['final'] = final
                    return res

                pend_prefix = None
                for h in range(8):
                    qT, kT, v, srg = qTs.next(), kTs.next(), vs.next(), srgs.next()
                    S.dma('sp', qT.t[:], RQ[h].rearrange('c p t -> p c t'), writes=[qT])
                    S.dma('sp', kT.t[:], RK[h].rearrange('c p t -> p c t'), writes=[kT])
                    S.dma('sp', v.t[:], RV[:, h * 256:(h + 1) * 256].rearrange('(b p) c -> p b c', p=128), writes=[v])
                    S.dma('sp', srg.t[:], SRG[:, h * 256:(h + 1) * 256].rearrange('(b p) c -> p b c', p=128), writes=[srg])
                    c2 = gam[h] ** 128
                    if h == 0:
                        pend_prefix = make_prefix(0)
                        for f in pend_prefix['steps']:
                            f()
                        pend_prefix['final']()
                    S32, Sb0 = pend_prefix['S32'], pend_prefix['Sb0']
                    nxt_prefix = make_prefix(h + 1) if h + 1 < 8 else None

                    def front(b):
                        bsl = slice(b * 128, (b + 1) * 128)
                        sp_, kp_ = sT_ps.next(), kd_ps.next()
                        sT, qd, kd = sTs.next(), qds.next(), kds.next()
                        for c in range(2):
                            S.op('pe', lambda e: e.matmul(sp_.t[:, 0:128], kT.t[:, c, bsl], qT.t[:, c, bsl], start=(c == 0), stop=(c == 1)),
                                 reads=[kT, qT], writes=[sp_])
                        for c in range(2):
                            S.op('pe', lambda e: e.transpose(kp_.t[:, c * 128:(c + 1) * 128], kT.t[:, c, bsl], ident.t[:, 0:128]),
                                 reads=[kT, ident], writes=[kp_])
                        S.op('dve', lambda e: e.tensor_tensor(out=sT.t[:], in0=sp_.t[:, 0:128], in1=rmask.t[:, h, :], op=ALU.mult),
                             reads=[sp_, rmask], writes=[sT])
                        S.op('pool', lambda e: e.tensor_tensor(out=qd.t[:].rearrange('p (c i) -> p c i', c=2), in0=qT.t[:, :, bsl],
                                                               in1=dq.t[:, h, :].rearrange('p (c i) -> p c i', c=2), op=ALU.mult),
                             reads=[qT, dq], writes=[qd])
                        S.op('act', lambda e: e.activation(out=kd.t[:], in_=kp_.t[:, 0:256], func=AF.Identity, scale=dk.t[:, h:h + 1]),
                             reads=[kp_, dk], writes=[kd])
                        return sT, qd, kd

                    state = {'Sb': Sb0}

                    def mid_back(b, sT, qd, kd, stage_buf):
                        op_ = o_ps.next()
                        Sb = state['Sb']
                        osb, onb, ss, s2, rsd = osbs.next(), onbs.next(), sss.next(), s2s.next(), rsds.next()
                        if b < NB - 1:
                            for c in range(2):
                                S.op('pe', lambda e: e.matmul(S_ps.t[:, c * 256:(c + 1) * 256], kd.t[:, c * 128:(c + 1) * 128], v.t[:, b, :], start=True, stop=True),
                                     reads=[kd, v], writes=[S_ps])
                        S.op('pe', lambda e: e.matmul(op_.t[:, 0:256], sT.t[:], v.t[:, b, :], start=True, stop=False),
                             reads=[sT, v], writes=[op_])
                        for c in range(2):
                            S.op('pe', lambda e: e.matmul(op_.t[:, 0:256], qd.t[:, c * 128:(c + 1) * 128], Sb.t[:, c, :], start=False, stop=(c == 1)),
                                 reads=[qd, Sb], writes=[op_])
                        if b < NB - 1:
                            S.op('dve', lambda e: e.scalar_tensor_tensor(out=S32.t[:], in0=S32.t[:], scalar=c2, in1=S_ps.t[:],
                                                                         op0=ALU.mult, op1=ALU.add),
                                 reads=[S32, S_ps], writes=[S32])
                            Sb2 = Sbs.next()
                            S.op('act', lambda e: e.activation(out=Sb2.t[:].rearrange('p c v -> p (c v)'), in_=S32.t[:], func=AF.Copy),
                                 reads=[S32], writes=[Sb2])
                            state['Sb'] = Sb2
                        if state.get('tail') is not None:
                            state['tail']()
                            state['tail'] = None
                        ab = abs_.next()
                        S.op('act', lambda e: e.activation(out=osb.t[:], in_=op_.t[:, 0:256], func=AF.Copy), reads=[op_], writes=[osb])
                        S.op('act', lambda e: e.activation(out=onb.t[:], in_=osb.t[:], func=AF.Square, accum_out=ss.t[:]),
                             reads=[osb], writes=[onb, ss])
                        S.op('dve', lambda e: e.tensor_scalar(out=s2.t[:], in0=ss.t[:], scalar1=1.0 / 256, scalar2=EPS, op0=ALU.mult, op1=ALU.add),
                             reads=[ss], writes=[s2])
                        S.op('pool', lambda e: e.tensor_tensor(out=rsd.t[:], in0=s2.t[:], in1=mhalf.t[:], op=ALU.pow),
                             reads=[s2, mhalf], writes=[rsd])
                        S.op('dve', lambda e: e.scalar_tensor_tensor(out=onb.t[:], in0=osb.t[:], scalar=rsd.t[:], in1=gret.t[:, h * 256:(h + 1) * 256],
                                                                     op0=ALU.mult, op1=ALU.mult),
                             reads=[osb, rsd, gret], writes=[onb])
                        S.op('pool', lambda e: e.tensor_tensor(out=ab.t[:], in0=onb.t[:], in1=srg.t[:, b, :], op=ALU.mult),
                             reads=[onb, srg], writes=[ab])

                        def tail(b=b, ab=ab, stage_buf=stage_buf, h=h):
                            for c in range(2):
                                S.op('pe', lambda e: e.transpose(aT_ps.t[:, c * 128:(c + 1) * 128], ab.t[:, c * 128:(c + 1) * 128], ident.t[:, 0:128]),
                                     reads=[ab, ident], writes=[aT_ps])
                            bb = b % 4
                            S.op('act', lambda e: e.activation(out=stage_buf.t[:, :, bb * 128:(bb + 1) * 128],
                                                               in_=aT_ps.t[:, 0:256].rearrange('p (c i) -> p c i', c=2), func=AF.Copy),
                                 reads=[aT_ps], writes=[stage_buf])
                            if bb == 3:
                                t0 = (b // 4) * 512
                                S.dma('sp', ART[2 * h:2 * h + 2, :, t0:t0 + 512].rearrange('c p t -> p c t'), stage_buf.t[:], reads=[stage_buf])
                        state['tail'] = tail

                    nxt = front(0)
                    stage_buf = None
                    for b in range(NB):
                        cur = nxt
                        if b + 1 < NB:
                            nxt = front(b + 1)
                        if b % 4 == 0:
                            stage_buf = aTst.next()
                        mid_back(b, cur[0], cur[1], cur[2], stage_buf)
                        if nxt_prefix is not None:
                            nxt_prefix['steps'][b]()
                    state['tail']()
                    state['tail'] = None
                    if nxt_prefix is not None:
                        nxt_prefix['final']()
                    pend_prefix = nxt_prefix
            S.barrier()
            with contextlib.ExitStack() as p4:
                kaT = sb(p4, 'kaT', [128, TF], BF16)
                S.dma('sp', kaT.t[:, 0:T], KAp, writes=[kaT])
                S.dma('sp', kaT.t[:, T:TF], KA, writes=[kaT])
                ikT = sb(p4, 'ikT', [128, TF], BF16)
                S.dma('sp', ikT.t[:, 0:T], IKp, writes=[ikT])
                S.dma('sp', ikT.t[:, T:TF], IK, writes=[ikT])
                av = sb(p4, 'av', [128, 2 * NB, 128], BF16)
                S.dma('sp', av.t[:, 0:NB, :], AVp.rearrange('(b p) c -> p b c', p=128), writes=[av])
                S.dma('sp', av.t[:, NB:2 * NB, :], AV.rearrange('(b p) c -> p b c', p=128), writes=[av])
                inadm = sb(p4, 'inadm', [128, 128], F32)
                S.dma('sp', inadm.t[:], inadm_d, writes=[inadm])
                p2t = sb(p4, 'p2t', [128, NIT], F32)
                S.dma('sp', p2t.t[:], p2_d, writes=[p2t])
                qaTs = Rot([sb(p4, 'qaT%d' % i, [128, 16, 128], BF16) for i in range(2)])
                iqTs = Rot([sb(p4, 'iqz%d' % i, [128, 2, 8, 128], BF16) for i in range(2)])
                for _b in iqTs.bufs:
                    S.op('dve', lambda e: e.memset(_b.t[:], 0.0), writes=[_b])
                iws = Rot([sb(p4, 'iw%d' % i, [128, 16], F32) for i in range(2)])
                sags = Rot([sb(p4, 'sag%d' % i, [128, 16, 128], BF16) for i in range(2)])
                Dg = sb(p4, 'Dg', [128, 16, 128], BF16)
                score = sb(p4, 'score', [128, TF], F32)
                junk = sb(p4, 'junk', [128, TF], BF16)
                negm = sb(p4, 'negm', [128, TF], BF16)
                rl = Rot([sb(p4, 'rl%d' % i, [128, 512], BF16) for i in range(4)])
                pTs = Rot([sb(p4, 'pT%d' % i, [128, 512], BF16) for i in range(4)])
                mx = sb(p4, 'mx', [128, 1], F32)
                mn = sb(p4, 'mn', [128, 1], F32)
                rng = sb(p4, 'rng', [128, 1], F32)
                th = sb(p4, 'th', [128, 1], F32)
                steps = sb(p4, 'steps', [128, NIT], F32)
                cnt = sb(p4, 'cnt', [128, 1], F32)
                sg = sb(p4, 'sg', [128, 1], F32)
                rd = sb(p4, 'rd', [128, 512], F32)
                tq = sb(p4, 'tq', [128, 512], F32)
                aast = Rot([sb(p4, 'aast%d' % i, [128, 16, 512], BF16) for i in range(2)])
                sl_ps = Rot([ps(p4, 'slp%d' % i, [128, 512]) for i in range(4)])
                acc_ps = Rot([ps(p4, 'accp%d' % i, [128, 512]) for i in range(2)])
                oo_ps = [ps(p4, 'oop%d' % i, [128, 512]) for i in range(1)]
                dd_ps = [ps(p4, 'ddp%d' % i, [128, 512]) for i in range(1)]

                negms = Rot([negm, sb(p4, 'negm2', [128, TF], BF16)])
                mn2 = sb(p4, 'mn2', [128, 1], F32)
                negm_of = {}

                def idx_phase(qb):
                    qsl = slice(qb * 128, (qb + 1) * 128)
                    Lk = T + (qb + 1) * 128
                    iqT, iw = iqTs.next(), iws.next()
                    ng = negms.next()
                    negm_of[qb] = ng
                    S.dma('sp', iqT.t[0:64, 0, :, :], IQ[:, 0:64, qsl].rearrange('h p t -> p h t'), writes=[iqT])
                    S.dma('sp', iqT.t[64:128, 1, :, :], IQ[:, 64:128, qsl].rearrange('h p t -> p h t'), writes=[iqT])
                    S.dma('sp', iw.t[:], IW[qsl, :], writes=[iw])
                    for hh in range(16):
                        S.op('pool', lambda e: e.tensor_scalar(out=Dg.t[:, hh, :], in0=ident.t[:, 0:128], scalar1=iw.t[:, hh:hh + 1],
                                                               scalar2=0.0, op0=ALU.mult, op1=ALU.add),
                             reads=[ident, iw], writes=[Dg])
                    for k0 in range(0, Lk, 512):
                        wk = min(512, Lk - k0)
                        acc = acc_ps.next()

                        def sc_mm(hh):
                            scp = sl_ps.next()
                            S.op('pe', lambda e: e.matmul(scp.t[:, 0:wk], iqT.t[:, hh % 2, hh // 2, :], ikT.t[:, k0:k0 + wk],
                                                         start=True, stop=True),
                                 reads=[iqT, ikT], writes=[scp])
                            return scp
                        LA = 2
                        pend = [sc_mm(hh) for hh in range(min(LA, 16))]
                        for hh in range(16):
                            scp = pend.pop(0)
                            if hh + LA < 16:
                                pend.append(sc_mm(hh + LA))
                            r = rl.next()
                            S.op('act', lambda e: e.activation(out=r.t[:, 0:wk], in_=scp.t[:, 0:wk], func=AF.Relu),
                                 reads=[scp], writes=[r])
                            S.op('pe', lambda e: e.matmul(acc.t[:, 0:wk], Dg.t[:, hh, :], r.t[:, 0:wk], start=(hh == 0), stop=(hh == 15)),
                                 reads=[Dg, r], writes=[acc])
                        dcol = T + qb * 128
                        c_lo, c_hi = k0, k0 + wk
                        p_hi = min(c_hi, T)
                        if p_hi > c_lo:
                            S.op('dve', lambda e: e.tensor_scalar(out=score.t[:, c_lo:p_hi], in0=acc.t[:, 0:p_hi - c_lo], scalar1=flags.t[:, 1:2],
                                                                  scalar2=None, op0=ALU.add),
                                 reads=[acc, flags], writes=[score])
                        o_lo = max(c_lo, T)
                        o_hi = min(c_hi, dcol)
                        if o_hi > o_lo:
                            S.op('dve', lambda e: e.tensor_copy(out=score.t[:, o_lo:o_hi], in_=acc.t[:, o_lo - k0:o_hi - k0]),
                                 reads=[acc], writes=[score])
                        if c_lo <= dcol < c_hi:
                            S.op('dve', lambda e: e.tensor_tensor(out=score.t[:, dcol:dcol + 128], in0=acc.t[:, dcol - k0:dcol - k0 + 128],
                                                                  in1=inadm.t[:], op=ALU.add),
                                 reads=[acc, inadm], writes=[score])
                    def bis_iters(i0, i1):
                        for it in range(i0, i1):
                            S.op('dve', lambda e: e.tensor_scalar(out=junk.t[:, 0:Lk], in0=score.t[:, 0:Lk], scalar1=th.t[:], scalar2=None,
                                                                  op0=ALU.is_ge, op1=ALU.add, accum_out=cnt.t[:]),
                                 reads=[score, th], writes=[junk, cnt])
                            S.op('dve', lambda e: e.tensor_scalar(out=sg.t[:], in0=cnt.t[:], scalar1=KTOP - 0.5, scalar2=0.5, op0=ALU.is_ge, op1=ALU.subtract),
                                 reads=[cnt], writes=[sg])
                            S.op('dve', lambda e: e.scalar_tensor_tensor(out=th.t[:], in0=sg.t[:], scalar=steps.t[:, it:it + 1], in1=th.t[:],
                                                                         op0=ALU.mult, op1=ALU.add),
                                 reads=[sg, steps, th], writes=[th])

                    def bis_init():
                        S.op('dve', lambda e: e.tensor_reduce(out=mx.t[:], in_=score.t[:, 0:Lk], axis=AX.X, op=ALU.max),
                             reads=[score], writes=[mx])
                        S.op('dve', lambda e: e.tensor_scalar(out=junk.t[:, 0:T], in0=score.t[:, 0:T], scalar1=flags.t[:, 2:3], scalar2=None,
                                                              op0=ALU.add, op1=ALU.min, accum_out=mn.t[:]),
                             reads=[score, flags], writes=[junk, mn])
                        if qb > 0:
                            S.op('dve', lambda e: e.tensor_reduce(out=mn2.t[:], in_=score.t[:, T:T + qb * 128], axis=AX.X, op=ALU.min),
                                 reads=[score], writes=[mn2])
                            S.op('dve', lambda e: e.tensor_tensor(out=mn.t[:], in0=mn.t[:], in1=mn2.t[:], op=ALU.min),
                                 reads=[mn, mn2], writes=[mn])
                        S.op('dve', lambda e: e.tensor_tensor(out=mn.t[:], in0=mn.t[:], in1=mx.t[:], op=ALU.min),
                             reads=[mn, mx], writes=[mn])
                        S.op('dve', lambda e: e.tensor_tensor(out=rng.t[:], in0=mx.t[:], in1=mn.t[:], op=ALU.subtract),
                             reads=[mx, mn], writes=[rng])
                        S.op('dve', lambda e: e.scalar_tensor_tensor(out=th.t[:], in0=rng.t[:], scalar=0.5, in1=mn.t[:], op0=ALU.mult, op1=ALU.add),
                             reads=[rng, mn], writes=[th])
                        S.op('dve', lambda e: e.tensor_scalar(out=steps.t[:], in0=p2t.t[:], scalar1=rng.t[:], scalar2=None, op0=ALU.mult),
                             reads=[p2t, rng], writes=[steps])

                    def bis_final():
                        S.op('dve', lambda e: e.tensor_scalar(out=junk.t[:, 0:Lk], in0=score.t[:, 0:Lk], scalar1=-1.0e29, scalar2=None,
                                                              op0=ALU.is_gt, op1=ALU.add, accum_out=cnt.t[:]),
                             reads=[score], writes=[junk, cnt])
                        S.op('dve', lambda e: e.tensor_scalar(out=sg.t[:], in0=cnt.t[:], scalar1=KTOP + 0.5, scalar2=None, op0=ALU.is_ge),
                             reads=[cnt], writes=[sg])
                        S.op('dve', lambda e: e.tensor_scalar(out=mx.t[:], in0=sg.t[:], scalar1=1.0, scalar2=1.0e29, op0=ALU.subtract, op1=ALU.mult),
                             reads=[sg], writes=[mx])
                        S.op('dve', lambda e: e.scalar_tensor_tensor(out=th.t[:], in0=th.t[:], scalar=sg.t[:], in1=mx.t[:], op0=ALU.mult, op1=ALU.add),
                             reads=[th, sg, mx], writes=[th])
                        S.op('dve', lambda e: e.tensor_scalar(out=ng.t[:, 0:Lk], in0=score.t[:, 0:Lk], scalar1=th.t[:], scalar2=NEG,
                                                              op0=ALU.is_lt, op1=ALU.mult),
                             reads=[score, th], writes=[ng])

                    q = NIT // 4
                    return [lambda: (bis_init(), bis_iters(0, q)), lambda: bis_iters(q, 2 * q),
                            lambda: bis_iters(2 * q, 3 * q), lambda: (bis_iters(3 * q, NIT), bis_final())]

                stage_of = {}

                def att_phase(qb, chunks):
                    qsl = slice(qb * 128, (qb + 1) * 128)
                    nkt = NB + qb + 1
                    ng = negm_of.pop(qb)
                    qaT, sag = qaTs.next(), sags.next()
                    S.dma('sp', qaT.t[:], QA[:, :, qsl].rearrange('h p t -> p h t'), writes=[qaT])
                    S.dma('sp', sag.t[:], SAG[:, :, qsl].rearrange('h p t -> p h t'), writes=[sag])
                    if qb % 4 == 0:
                        stage_of['s'] = aast.next()
                    stage = stage_of['s']
                    q4 = qb % 4
                    for hg in range(4):
                        chunks[hg]()
                        units = list(range(nkt))

                        def logits(kt):
                            ksl = slice(kt * 128, (kt + 1) * 128)
                            lp = sl_ps.next()
                            S.op('pe', lambda e: e.matmul(lp.t[:], kaT.t[:, ksl], qaT.t[:, 4 * hg:4 * hg + 4, :].rearrange('p h t -> p (h t)'),
                                                         start=True, stop=False),
                                 reads=[kaT, qaT], writes=[lp])
                            S.op('pe', lambda e: e.matmul(lp.t[:], ng.t[:, ksl], ident.t[:], start=False, stop=True),
                                 reads=[ng, ident], writes=[lp])
                            return lp
                        LA = 2
                        pend = [logits(kt) for kt in units[:LA]]
                        for kt in units:
                            lp = pend.pop(0)
                            if kt + LA < nkt:
                                pend.append(logits(kt + LA))
                            pT = pTs.next()
                            S.op('act', lambda e: e.activation(out=pT.t[:], in_=lp.t[:], func=AF.Exp, scale=ATT_SCALE),
                                 reads=[lp], writes=[pT])
                            S.op('pe', lambda e: e.matmul(oo_ps[0].t[:], av.t[:, kt, :], pT.t[:], start=(kt == 0), stop=(kt == nkt - 1)),
                                 reads=[av, pT], writes=[oo_ps[0]])
                            S.op('pe', lambda e: e.matmul(dd_ps[0].t[:], ones_b.t[:], pT.t[:], start=(kt == 0), stop=(kt == nkt - 1)),
                                 reads=[ones_b, pT], writes=[dd_ps[0]])
                        S.op('dve', lambda e: e.reciprocal(out=rd.t[:], in_=dd_ps[0].t[:]), reads=[dd_ps[0]], writes=[rd])
                        S.op('dve', lambda e: e.tensor_tensor(out=tq.t[:], in0=oo_ps[0].t[:], in1=rd.t[:], op=ALU.mult),
                             reads=[oo_ps[0], rd], writes=[tq])
                        S.op('pool', lambda e: e.tensor_tensor(out=stage.t[:, 4 * hg:4 * hg + 4, q4 * 128:(q4 + 1) * 128],
                                                               in0=tq.t[:].rearrange('p (h t) -> p h t', h=4),
                                                               in1=sag.t[:, 4 * hg:4 * hg + 4, :], op=ALU.mult),
                             reads=[tq, sag], writes=[stage])
                    if q4 == 3:
                        t0 = (qb // 4) * 512
                        S.dma('sp', AAT[:, :, t0:t0 + 512].rearrange('c p t -> p c t'), stage.t[:], reads=[stage])

                for f in idx_phase(0):
                    f()
                for qb in range(NB):
                    chunks = idx_phase(qb + 1) if qb + 1 < NB else [lambda: None] * 4
                    att_phase(qb, chunks)
            S.barrier()
            with contextlib.ExitStack() as p5:
                bufAs = [sb(p5, 'bufA%d' % i, [128, KC, T], BF16) for i in range(2)]
                S.dma('sp', bufAs[0].t[:], ART.rearrange('c p t -> p c t'), writes=[bufAs[0]])
                S.dma('sp', bufAs[1].t[:], AAT.rearrange('c p t -> p c t'), writes=[bufAs[1]])
                bufM = sb(p5, 'bufM', [128, KC, T], BF16)
                gts = Rot([sb(p5, 'gt%d' % i, [128, T], BF16) for i in range(2)])
                tm5 = Rot([sb(p5, 't5%d' % i, [128, 512], F32) for i in range(2)])
                xts = Rot([sb(p5, 'x5%d' % i, [128, 512], F32) for i in range(3)])
                xos = Rot([sb(p5, 'xo%d' % i, [128, 512], F32) for i in range(3)])
                pb5 = Rot([ps(p5, 'p5%d' % i, [128, 512]) for i in range(4)])
                for br, (Asrc, wsrc, Gsrc) in enumerate(((ART, w_br[l], SGA), (AAT, w_ba[l], SGB))):
                    bufA = bufAs[br]
                    for wp in range(8):
                        wt = load_w(wsrc, wp * 256)
                        for mh in range(2):
                            mt = wp * 2 + mh
                            gt = gts.next()
                            S.dma('sp', gt.t[:], Gsrc[mt], writes=[gt])
                            for tt in range(NT):
                                tsl = slice(tt * 512, (tt + 1) * 512)
                                pbuf = pb5.next()
                                for kc in range(KC):
                                    S.op('pe', lambda e: e.matmul(pbuf.t[:], wt.t[:, kc, mh * 128:(mh + 1) * 128], bufA.t[:, kc, tsl],
                                                                 start=(kc == 0), stop=(kc == KC - 1)),
                                         reads=[wt, bufA], writes=[pbuf])
                                if br == 0:
                                    S.op('dve', lambda e: e.tensor_tensor(out=bufM.t[:, mt, tsl], in0=pbuf.t[:], in1=gt.t[:, tsl], op=ALU.mult),
                                         reads=[pbuf, gt], writes=[bufM])
                                else:
                                    t5 = tm5.next()
                                    S.op('dve', lambda e: e.tensor_tensor(out=t5.t[:], in0=pbuf.t[:], in1=gt.t[:, tsl], op=ALU.mult),
                                         reads=[pbuf, gt], writes=[t5])
                                    S.op('pool', lambda e: e.tensor_tensor(out=bufM.t[:, mt, tsl], in0=t5.t[:], in1=bufM.t[:, mt, tsl], op=ALU.add),
                                         reads=[t5, bufM], writes=[bufM])
                for wp in range(8):
                    wt = load_w(w_o[l], wp * 256)
                    for mh in range(2):
                        mt = wp * 2 + mh
                        for tt in range(NT):
                            tsl = slice(tt * 512, (tt + 1) * 512)
                            xt5 = xts.next()
                            S.dma('sp', xt5.t[:], x_src[mt * 128:(mt + 1) * 128, tsl], writes=[xt5])
                            pbuf = pb5.next()
                            for kc in range(KC):
                                S.op('pe', lambda e: e.matmul(pbuf.t[:], wt.t[:, kc, mh * 128:(mh + 1) * 128], bufM.t[:, kc, tsl],
                                                             start=(kc == 0), stop=(kc == KC - 1)),
                                     reads=[wt, bufM], writes=[pbuf])
                            xo = xos.next()
                            S.op('dve', lambda e: e.tensor_tensor(out=xo.t[:], in0=pbuf.t[:], in1=xt5.t[:], op=ALU.add),
                                 reads=[pbuf, xt5], writes=[xo])
                            S.dma('sp', x_dst[mt * 128:(mt + 1) * 128, tsl], xo.t[:], reads=[xo])
            S.barrier()
    return nc


def _consts(TF, L, norm_g, ret_out_g, att_q_g, att_k_g, idx_k_g, rank):
    f32 = np.float32
    T = TF // 2
    pos = np.arange(rank * T, (rank + 1) * T).astype(f32)
    inv = (1.0 / (np.float32(10000.0) ** np.linspace(0.0, 1.0, 128, dtype=f32))).astype(f32)
    ang = (pos[:, None] * inv[None, :]).astype(f32)
    cs = np.stack([np.cos(ang).T, np.sin(ang).T]).astype(f32)
    gam = 1.0 - 2.0 ** (-5.0 - np.arange(8, dtype=np.float64))
    i = np.arange(128)
    ii, jj = i[:, None], i[None, :]
    same = (ii // 64) == (jj // 64)
    later = (ii // 64) > (jj // 64)
    rmask = np.zeros((128, 8, 128), np.float64)
    for h in range(8):
        m = np.where(same, gam[h] ** np.abs(ii - jj), np.where(later, gam[h] ** (ii - jj), 0.0))
        rmask[:, h, :] = m.T / 16.0
    dq = np.zeros((128, 8, 256), np.float64)
    for h in range(8):
        dq[:, h, :] = np.tile(gam[h] ** (i + 1.0), 2)[None, :]
    dk = np.stack([gam[h] ** (127.0 - i) / 16.0 for h in range(8)], axis=1)
    nb = T // 128
    dkp = np.zeros((128, 8, nb), np.float64)
    for h in range(8):
        for b in range(nb):
            dkp[:, h, b] = gam[h] ** (T - 1.0 - (b * 128 + i)) / 16.0
    flags = np.zeros((128, 4), np.float64)
    flags[:, 0] = float(rank)
    flags[:, 1] = 0.0 if rank == 1 else -1.0e30
    flags[:, 2] = 0.0 if rank == 1 else 2.0e30
    ident = np.tile(np.eye(128, dtype=f32), (1, 4))
    inadm = np.where((jj >= 64) & (ii < 64), -1.0e30, 0.0).astype(f32)
    p2tab = np.tile((2.0 ** -(np.arange(NIT) + 1.0))[None, :], (128, 1)).astype(f32)
    gcol = norm_g.reshape(L, 16, 128).transpose(2, 0, 1).reshape(128, L * 16)
    gret = np.broadcast_to(ret_out_g.reshape(1, L * 2048), (128, L * 2048))
    gqk = np.stack([att_q_g, att_k_g, np.concatenate([idx_k_g, idx_k_g], axis=1)], axis=1)
    gqk = gqk.transpose(2, 0, 1).reshape(128, L * 3)
    c = lambda a: np.ascontiguousarray(a, dtype=f32)
    return {'cs': c(cs), 'rmask': c(rmask.reshape(128, 1024)), 'dq': c(dq.reshape(128, 2048)), 'dk': c(dk),
            'dkp': c(dkp.reshape(128, 8 * nb)), 'flags': c(flags),
            'ident': c(ident), 'inadm': c(inadm), 'p2tab': c(p2tab), 'gcol': c(gcol), 'gret': c(gret), 'gqk': c(gqk)}


def run_layers(x_list, layers, norm_g, w_in, ret_out_g, att_q_g, att_k_g, idx_k_g, w_branch_ret, w_branch_att, w_out):
    L = len(layers)
    TF = x_list[0].shape[0]
    T = TF // 2
    sl = np.array(layers)
    shared = {'w_in': np.ascontiguousarray(w_in[sl]), 'w_br': np.ascontiguousarray(w_branch_ret[sl]),
              'w_ba': np.ascontiguousarray(w_branch_att[sl]), 'w_o': np.ascontiguousarray(w_out[sl])}
    consts = [_consts(TF, L, norm_g[sl], ret_out_g[sl], att_q_g[sl], att_k_g[sl], idx_k_g[sl], r) for r in range(2)]
    nc = build(L, TF)
    in_maps = []
    for x in x_list:
        for r in range(2):
            m = dict(shared)
            m.update(consts[r])
            m['xT'] = np.ascontiguousarray(x[r * T:(r + 1) * T].T, dtype=np.float32)
            in_maps.append(m)
    res = run_bass_kernel_spmd(nc, in_maps, core_ids=list(range(len(in_maps))))
    outs = []
    for i in range(len(x_list)):
        outs.append(np.concatenate([res.results[2 * i]['outT'].T, res.results[2 * i + 1]['outT'].T], axis=0))
    return outs


def kernel(x, norm_g, w_in, ret_out_g, att_q_g, att_k_g, idx_k_g, w_branch_ret, w_branch_att, w_out):
    x = np.asarray(x, dtype=np.float32)
    B = x.shape[0]
    depth = w_in.shape[0]
    args = [np.asarray(a, dtype=np.float32) for a in
            (norm_g, w_in, ret_out_g, att_q_g, att_k_g, idx_k_g, w_branch_ret, w_branch_att, w_out)]
    outs = run_layers([x[b] for b in range(B)], list(range(depth)), *args)
    return np.stack(outs, axis=0).astype(np.float32)
```
